# Optimizing a Trainium2 kernel written in Bass

```python
import jax, jax.numpy as jnp
from jax import lax
import numpy as np

D_MODEL = 2048
BATCH = 2
SEQ = 8192
DEPTH = 2

HEAD_DIM = 128
BLOCK = 128
ROPE_THETA = 10000.0
EPS = 1e-6
NEG_INF = -1e30

SB_HEADS = 4
SB_WIDTH = SB_HEADS * HEAD_DIM
DIL_PATTERNS = ((128, 1), (512, 4), (2048, 16))
DIL_HEADS_PER_GROUP = 2
DIL_HEADS = DIL_HEADS_PER_GROUP * len(DIL_PATTERNS)
DIL_WIDTH = DIL_HEADS * HEAD_DIM
DIL_OUT_WIDTH = DIL_HEADS_PER_GROUP * HEAD_DIM
MLA_HEADS = 6
MLA_NOPE = 128
MLA_ROPE = 64
MLA_V = 128
MLA_QK = MLA_NOPE + MLA_ROPE
MLA_Q_RANK = 512
MLA_KV_RANK = 256
N_BRANCH = 3
IN_SPLITS = (SB_WIDTH, SB_WIDTH, SB_WIDTH, DIL_WIDTH, DIL_WIDTH, DIL_WIDTH,
             MLA_Q_RANK, MLA_KV_RANK, MLA_ROPE, N_BRANCH * D_MODEL)
IN_COLS = 3 * SB_WIDTH + 3 * DIL_WIDTH + MLA_Q_RANK + MLA_KV_RANK + MLA_ROPE + N_BRANCH * D_MODEL
PEER_HEADS = 8
PEER_NKEYS = 128
PEER_EXPERTS = PEER_NKEYS * PEER_NKEYS
PEER_QDIM = 256
PEER_HALF = PEER_QDIM // 2
PEER_TOPK = 16
PEER_CHUNK = 128

kernel_name = "hybrid_sb_dilated_mla_peer"


def rmsnorm(x, g):
    xf = x.astype(jnp.float32)
    y = xf * lax.rsqrt(jnp.mean(xf * xf, axis=-1, keepdims=True) + EPS)
    return (y * g.astype(jnp.float32)).astype(x.dtype)


def rope(x, positions):
    half = x.shape[-1] // 2
    inv = ROPE_THETA ** (-jnp.arange(half, dtype=jnp.float32) / half)
    ang = positions.astype(jnp.float32)[..., None] * inv
    cos = jnp.cos(ang)[:, :, None, :]
    sin = jnp.sin(ang)[:, :, None, :]
    xf = x.astype(jnp.float32)
    x1, x2 = xf[..., :half], xf[..., half:]
    return jnp.concatenate([x1 * cos - x2 * sin, x2 * cos + x1 * sin], axis=-1).astype(x.dtype)


def _split(p, sizes):
    offs, acc = [], 0
    for s_ in sizes[:-1]:
        acc += s_
        offs.append(acc)
    return jnp.split(p, offs, axis=-1)


def to_blocks(a):
    B, S, H, d = a.shape
    return a.reshape(B, S // BLOCK, BLOCK, H, d).transpose(1, 0, 2, 3, 4)


def from_blocks(a):
    nb, B, blk, H, d = a.shape
    return a.transpose(1, 0, 2, 3, 4).reshape(B, nb * blk, H, d)


def stick_breaking_attention(q, k, v):
    B, S, H, dh = q.shape
    scale = dh ** -0.5
    kpos = jnp.arange(S)

    def one_block(args):
        qb, i = args
        qpos = i * BLOCK + jnp.arange(BLOCK)
        z = jnp.einsum('bqhd,bkhd->bhqk', qb, k, preferred_element_type=jnp.float32) * scale
        mask = kpos[None, :] < qpos[:, None]
        log_beta = jax.nn.log_sigmoid(z)
        log_keep = jnp.where(mask, jax.nn.log_sigmoid(-z), 0.0)
        after = lax.cumsum(log_keep, axis=3, reverse=True) - log_keep
        w = jnp.where(mask, jnp.exp(log_beta + after), 0.0)
        return jnp.einsum('bhqk,bkhd->bqhd', w.astype(v.dtype), v)

    out = lax.map(one_block, (to_blocks(q), jnp.arange(S // BLOCK)))
    return from_blocks(out)


def causal_softmax_attention(q, k, v, scale):
    S = q.shape[1]
    kpos = jnp.arange(S)

    def one_block(args):
        qb, i = args
        qpos = i * BLOCK + jnp.arange(BLOCK)
        s = jnp.einsum('bqhd,bkhd->bhqk', qb, k, preferred_element_type=jnp.float32) * scale
        s = jnp.where(kpos[None, :] <= qpos[:, None], s, NEG_INF)
        p = jax.nn.softmax(s, axis=-1)
        return jnp.einsum('bhqk,bkhd->bqhd', p.astype(v.dtype), v)

    out = lax.map(one_block, (to_blocks(q), jnp.arange(S // BLOCK)))
    return from_blocks(out)


def dilated_group_attention(q, k, v, window, dilation):
    B, S, H, dh = q.shape
    n_back = window // dilation
    assert n_back <= BLOCK
    L = S // dilation
    nb = -(-L // BLOCK)
    Lp = nb * BLOCK

    def by_residue(a):
        a = a.reshape(B, L, dilation, H, dh).transpose(0, 2, 1, 3, 4)
        a = jnp.pad(a, ((0, 0), (0, 0), (0, Lp - L), (0, 0), (0, 0)))
        return a.reshape(B, dilation, nb, BLOCK, H, dh)

    def with_prev(a):
        prev = jnp.pad(a, ((0, 0), (0, 0), (1, 0), (0, 0), (0, 0), (0, 0)))[:, :, :-1]
        return jnp.concatenate([prev, a], axis=3)

    qr = by_residue(q)
    kw = with_prev(by_residue(k))
    vw = with_prev(by_residue(v))
    s = jnp.einsum('brnqhd,brnkhd->brnhqk', qr, kw, preferred_element_type=jnp.float32) * dh ** -0.5
    qi = jnp.arange(BLOCK)[:, None]
    kj = jnp.arange(2 * BLOCK)[None, :]
    dist = BLOCK + qi - kj
    band = (dist >= 0) & (dist <= n_back)
    valid = (jnp.arange(nb)[:, None, None] > 0) | (kj[None] >= BLOCK)
    mask = band[None] & valid
    s = jnp.where(mask[:, None], s, NEG_INF)
    m = jnp.max(s, axis=-1, keepdims=True)
    e = jnp.exp(s - m)
    den = jnp.sum(e, axis=-1, keepdims=True)
    o = jnp.einsum('brnhqk,brnkhd->brnqhd', (e / den).astype(v.dtype), vw)
    lse = (m + jnp.log(den))[..., 0]
    o = o.reshape(B, dilation, Lp, H, dh)[:, :, :L].transpose(0, 2, 1, 3, 4).reshape(B, S, H, dh)
    lse = lse.transpose(0, 1, 2, 4, 3).reshape(B, dilation, Lp, H)[:, :, :L]
    lse = lse.transpose(0, 2, 1, 3).reshape(B, S, H)
    return o, lse


def mla_attention(c_q, c_kv, k_rope, positions, cq_norm, ckv_norm, w_uq, w_ukv, q_norm, k_norm):
    B, S, _ = c_q.shape
    q = (rmsnorm(c_q, cq_norm) @ w_uq).reshape(B, S, MLA_HEADS, MLA_QK)
    kv = (rmsnorm(c_kv, ckv_norm) @ w_ukv).reshape(B, S, MLA_HEADS, MLA_NOPE + MLA_V)
    k_nope, v = kv[..., :MLA_NOPE], kv[..., MLA_NOPE:]
    k_pe = jnp.broadcast_to(k_rope[:, :, None, :], (B, S, MLA_HEADS, MLA_ROPE))
    k = jnp.concatenate([k_nope, k_pe], axis=-1)
    q = rmsnorm(q, q_norm)
    k = rmsnorm(k, k_norm)
    q = jnp.concatenate([q[..., :MLA_NOPE], rope(q[..., MLA_NOPE:], positions)], axis=-1)
    k = jnp.concatenate([k[..., :MLA_NOPE], rope(k[..., MLA_NOPE:], positions)], axis=-1)
    return causal_softmax_attention(q, k, v, MLA_QK ** -0.5)


def peer_ffn(u, w_query, sub_keys, expert_u, expert_v):
    B, S, D = u.shape
    T = B * S
    H, K = PEER_HEADS, PEER_TOPK
    ut = u.reshape(T, D)
    q = (ut @ w_query).reshape(T, H, 2, PEER_HALF)
    s = jnp.einsum('thpc,pnc->thpn', q, sub_keys, preferred_element_type=jnp.float32)
    sv, si = lax.top_k(s, K)
    cand = sv[:, :, 0, :, None] + sv[:, :, 1, None, :]
    cand_idx = si[:, :, 0, :, None] * PEER_NKEYS + si[:, :, 1, None, :]
    best, sel = lax.top_k(cand.reshape(T, H, K * K), K)
    idx = jnp.take_along_axis(cand_idx.reshape(T, H, K * K), sel, axis=-1).reshape(T, H * K)
    gate = jax.nn.softmax(best, axis=-1).reshape(T, H * K)
    nc = T // PEER_CHUNK

    def chunk(args):
        xc, ic, gc = args
        a = jnp.einsum('td,tkd->tk', xc, expert_u[ic])
        hidden = jax.nn.gelu(a, approximate=False) * gc.astype(a.dtype)
        return jnp.einsum('tk,tkd->td', hidden, expert_v[ic])

    y = lax.map(chunk, (ut.reshape(nc, PEER_CHUNK, D), idx.reshape(nc, PEER_CHUNK, H * K),
                        gate.reshape(nc, PEER_CHUNK, H * K)))
    return y.reshape(B, S, D)


def hybrid_layer(x, positions, norm_mix, w_in, dil_q_norm, dil_k_norm, mla_cq_norm, mla_ckv_norm,
                 mla_w_uq, mla_w_ukv, mla_q_norm, mla_k_norm, w_branch_sb, w_branch_dil,
                 w_branch_mla, w_out, norm_ffn, peer_w_query, peer_sub_keys, peer_u, peer_v):
    B, S, D = x.shape
    u = rmsnorm(x, norm_mix)
    proj = jnp.einsum('bsd,dc->bsc', u, w_in)
    (sb_q, sb_k, sb_v, d_q, d_k, d_v, c_q, c_kv, k_rope, gate_logits) = _split(proj, IN_SPLITS)

    def heads(a, n):
        return a.reshape(B, S, n, HEAD_DIM)

    o_sb = stick_breaking_attention(heads(sb_q, SB_HEADS), heads(sb_k, SB_HEADS),
                                    heads(sb_v, SB_HEADS)).reshape(B, S, SB_WIDTH)

    dq = rope(rmsnorm(heads(d_q, DIL_HEADS), dil_q_norm), positions)
    dk = rope(rmsnorm(heads(d_k, DIL_HEADS), dil_k_norm), positions)
    dv = heads(d_v, DIL_HEADS)
    outs, lses = [], []
    for g, (window, dilation) in enumerate(DIL_PATTERNS):
        sl = slice(g * DIL_HEADS_PER_GROUP, (g + 1) * DIL_HEADS_PER_GROUP)
        o_g, lse_g = dilated_group_attention(dq[:, :, sl], dk[:, :, sl], dv[:, :, sl], window, dilation)
        outs.append(o_g)
        lses.append(lse_g)
    alpha = jax.nn.softmax(jnp.stack(lses), axis=0)
    o_dil = jnp.einsum('gbsh,gbshd->bshd', alpha.astype(x.dtype), jnp.stack(outs)).reshape(B, S, DIL_OUT_WIDTH)

    o_mla = mla_attention(c_q, c_kv, k_rope, positions, mla_cq_norm, mla_ckv_norm, mla_w_uq, mla_w_ukv,
                          mla_q_norm, mla_k_norm).reshape(B, S, MLA_HEADS * MLA_V)

    gates = jax.nn.sigmoid(gate_logits.reshape(B, S, N_BRANCH, D))
    merged = (gates[:, :, 0] * (o_sb @ w_branch_sb)
              + gates[:, :, 1] * (o_dil @ w_branch_dil)
              + gates[:, :, 2] * (o_mla @ w_branch_mla))
    h = x + merged @ w_out

    return h + peer_ffn(rmsnorm(h, norm_ffn), peer_w_query, peer_sub_keys, peer_u, peer_v)


def setup_inputs(seed: int = 0) -> dict:
    key = jax.random.key(seed)
    ks = jax.random.split(key, 24)
    f32 = jnp.float32

    def w(k, shape, fan_in):
        return jax.random.normal(k, shape, f32) * (fan_in ** -0.5)

    def gain(k, n):
        return 1.0 + 0.02 * jax.random.normal(k, (DEPTH, n), f32)

    return {
        "x": jax.random.normal(ks[0], (BATCH, SEQ, D_MODEL), f32),
        "positions": jnp.broadcast_to(jnp.arange(SEQ, dtype=jnp.int32), (BATCH, SEQ)),
        "norm_mix": gain(ks[1], D_MODEL),
        "w_in": w(ks[2], (DEPTH, D_MODEL, IN_COLS), D_MODEL),
        "dil_q_norm": gain(ks[3], HEAD_DIM),
        "dil_k_norm": gain(ks[4], HEAD_DIM),
        "mla_cq_norm": gain(ks[5], MLA_Q_RANK),
        "mla_ckv_norm": gain(ks[6], MLA_KV_RANK),
        "mla_w_uq": w(ks[7], (DEPTH, MLA_Q_RANK, MLA_HEADS * MLA_QK), MLA_Q_RANK),
        "mla_w_ukv": w(ks[8], (DEPTH, MLA_KV_RANK, MLA_HEADS * (MLA_NOPE + MLA_V)), MLA_KV_RANK),
        "mla_q_norm": gain(ks[9], MLA_QK),
        "mla_k_norm": gain(ks[10], MLA_QK),
        "w_branch_sb": w(ks[11], (DEPTH, SB_WIDTH, D_MODEL), SB_WIDTH),
        "w_branch_dil": w(ks[12], (DEPTH, DIL_OUT_WIDTH, D_MODEL), DIL_OUT_WIDTH),
        "w_branch_mla": w(ks[13], (DEPTH, MLA_HEADS * MLA_V, D_MODEL), MLA_HEADS * MLA_V),
        "w_out": w(ks[14], (DEPTH, D_MODEL, D_MODEL), D_MODEL),
        "norm_ffn": gain(ks[15], D_MODEL),
        "peer_w_query": w(ks[16], (DEPTH, D_MODEL, PEER_HEADS * PEER_QDIM), D_MODEL),
        "peer_sub_keys": w(ks[17], (DEPTH, 2, PEER_NKEYS, PEER_HALF), PEER_HALF),
        "peer_u": w(ks[18], (DEPTH, PEER_EXPERTS, D_MODEL), D_MODEL),
        "peer_v": w(ks[19], (DEPTH, PEER_EXPERTS, D_MODEL), PEER_HEADS * PEER_TOPK),
    }


def reference(x, positions, norm_mix, w_in, dil_q_norm, dil_k_norm, mla_cq_norm, mla_ckv_norm,
              mla_w_uq, mla_w_ukv, mla_q_norm, mla_k_norm, w_branch_sb, w_branch_dil, w_branch_mla,
              w_out, norm_ffn, peer_w_query, peer_sub_keys, peer_u, peer_v):
    for l in range(DEPTH):
        x = hybrid_layer(x, positions, norm_mix[l], w_in[l], dil_q_norm[l], dil_k_norm[l],
                         mla_cq_norm[l], mla_ckv_norm[l], mla_w_uq[l], mla_w_ukv[l],
                         mla_q_norm[l], mla_k_norm[l], w_branch_sb[l], w_branch_dil[l],
                         w_branch_mla[l], w_out[l], norm_ffn[l], peer_w_query[l],
                         peer_sub_keys[l], peer_u[l], peer_v[l])
    return x
```

```python
import numpy as np
import concourse.bass as bass
import concourse.mybir as mybir
from concourse.bass_utils import run_bass_kernel_spmd
from contextlib import ExitStack

F32 = mybir.dt.float32
BF16 = mybir.dt.bfloat16
I32 = mybir.dt.int32
U32 = mybir.dt.uint32
ALU = mybir.AluOpType
AF = mybir.ActivationFunctionType
AX = mybir.AxisListType

ENGINES = ("sync", "scalar", "vector", "gpsimd", "tensor")


class Trk:
    __slots__ = ("name", "w", "rs", "dsem", "dcnt")

    def __init__(self, name):
        self.name = name
        self.w = None
        self.rs = []
        self.dsem = None
        self.dcnt = 0


class Buf:
    def __init__(self, name, t):
        self.name = name
        self.t = t
        self.trk = Trk(name)
        self.subs = {}

    def __getitem__(self, key):
        return self.t[key]

    def sub(self, key):
        if key not in self.subs:
            self.subs[key] = Trk("%s/%s" % (self.name, key))
        return self.subs[key]


def _trk(x):
    return x.trk if isinstance(x, Buf) else x


class Prog:
    def __init__(self, same_engine_sync=True):
        self.nc = bass.Bass("TRN2", target_bir_lowering=False)
        self.es = ExitStack()
        self.ops = {e: [] for e in ENGINES}
        self.cnt = {e: 0 for e in ENGINES}
        self.sems = {}
        self.waited = {e: {} for e in ENGINES}
        self.same_engine_sync = same_engine_sync
        self.n_dsem = 0
        self.out_trks = []
        for e in ENGINES:
            self.sems[("E", e)] = self.es.enter_context(self.nc.semaphore("s_" + e))

    def dram(self, name, shape, dtype, kind):
        return Buf(name, self.nc.dram_tensor(name, list(shape), dtype, kind=kind).ap())

    def sbuf(self, name, shape, dtype):
        return Buf(name, self.es.enter_context(self.nc.sbuf_tensor(name, list(shape), dtype)))

    def psum(self, name, shape, dtype):
        return Buf(name, self.es.enter_context(self.nc.psum_tensor(name, list(shape), dtype)))

    def _dsem(self, trk):
        if trk.dsem is None:
            trk.dsem = ("D", self.n_dsem)
            self.sems[trk.dsem] = self.es.enter_context(self.nc.semaphore("d%d" % self.n_dsem))
            self.n_dsem += 1
        return trk.dsem

    def _deps(self, eng, R, W):
        deps = []
        for r in R:
            if r.w is not None:
                deps.append(r.w)
        for w in W:
            if w.w is not None:
                deps.append(w.w)
            deps.extend(w.rs)
        need = {}
        for (k, v) in deps:
            if k == ("E", eng) and not self.same_engine_sync:
                continue
            if v > need.get(k, 0):
                need[k] = v
        waits = []
        wd = self.waited[eng]
        for k, v in need.items():
            if wd.get(k, 0) < v:
                wd[k] = v
                waits.append((k, v))
        return waits

    def op(self, eng, fn, R=(), W=(), pe_acc=False):
        R = [_trk(x) for x in R]
        W = [_trk(x) for x in W]
        waits = self._deps(eng, R, W)
        if pe_acc:
            waits = [(k, v) for (k, v) in waits if k != ("E", eng)]
        self.cnt[eng] += 1
        tk = (("E", eng), self.cnt[eng])
        self.ops[eng].append((waits, fn, (("E", eng), 1)))
        for r in R:
            r.rs.append(tk)
        for w in W:
            w.w = tk
            w.rs = []
        return tk

    def dma(self, eng, out, in_, R, W, sb, **kw):
        R = [_trk(x) for x in R]
        W = [_trk(x) for x in W]
        sb = _trk(sb)
        waits = self._deps(eng, R, W)
        k = self._dsem(sb)
        sb.dcnt += 16
        tk = (k, sb.dcnt)
        self.ops[eng].append((waits, lambda e: e.dma_start(out=out, in_=in_, **kw), (k, 16)))
        for r in R:
            r.rs.append(tk)
        for w in W:
            w.w = tk
            w.rs = []
        return tk

    def mark_output(self, trk):
        self.out_trks.append(_trk(trk))

    def build(self):
        fin = []
        for t in self.out_trks:
            if t.w is not None:
                fin.append(t.w)
        for e in ENGINES:
            if self.cnt[e] > 0 and e != "sync":
                fin.append((("E", e), self.cnt[e]))
        for k, h in self.sems.items():
            pass
        nc = self.nc
        ops = self.ops
        sems = self.sems
        with nc.Block() as block:
            def emit(engname):
                def body(e):
                    for (waits, fn, inc) in ops[engname]:
                        for (k, v) in waits:
                            e.wait_ge(sems[k], v)
                        ins = fn(e)
                        ins.then_inc(sems[inc[0]], inc[1])
                    if engname == "sync":
                        need = {}
                        for (k, v) in fin:
                            need[k] = max(need.get(k, 0), v)
                        for k, v in need.items():
                            e.wait_ge(sems[k], v)
                return body
            block.sync(emit("sync"))
            block.scalar(emit("scalar"))
            block.vector(emit("vector"))
            block.gpsimd(emit("gpsimd"))
            block.tensor(emit("tensor"))
        self.es.close()
        return nc


NIDX = 640
ARENA_BF16 = 102912


class Prog2(Prog):
    def __init__(self):
        Prog.__init__(self)
        self.arena = self.es.enter_context(self.nc.sbuf_tensor("arena", [128, ARENA_BF16], BF16))
        self.a_off = 0
        self.banks = [self.es.enter_context(self.nc.psum_tensor("bank%d" % i, [128, 512], F32)) for i in range(8)]
        self.b_off = 0
        self.pending = {e: [] for e in ENGINES}
        self.dtot = {}
        self.free_dsems = []
        self.stage_dsems = []
        self.specs = []
        self.cc_sem = self.es.enter_context(self.nc.semaphore("cc"))
        self.sems[("C", 0)] = self.cc_sem
        self.cc_cnt = 0
        self.uid = 0
        self.ever = 0

    def sbuf(self, name, shape, dtype):
        nbytes = {F32: 4, BF16: 2, I32: 4}[dtype]
        n = 1
        for s_ in shape[1:]:
            n *= s_
        nb16 = (n * nbytes + 1) // 2
        nb16 = (nb16 + 15) // 16 * 16
        assert self.a_off + nb16 <= ARENA_BF16, ("sbuf arena overflow", name, self.a_off, nb16)
        ap = self.arena[0:shape[0], self.a_off:self.a_off + (n * nbytes) // 2]
        self.a_off += nb16
        if dtype != BF16:
            ap = ap.bitcast(dtype)
        if len(shape) > 2:
            names = " ".join("d%d" % i for i in range(len(shape) - 1))
            kw = {"d%d" % i: shape[i + 1] for i in range(len(shape) - 1)}
            ap = ap.rearrange("p (%s) -> p %s" % (names, names), **kw)
        self.uid += 1
        return Buf("%s_%d" % (name, self.uid), ap)

    def psum(self, name, shape, dtype):
        b = self.banks[self.b_off]
        self.b_off += 1
        assert self.b_off <= 8
        if dtype == F32:
            ap = b[0:shape[0], 0:shape[1]]
        else:
            ap = b[:].bitcast(BF16)[0:shape[0], 0:shape[1]]
        self.uid += 1
        return Buf("%s_%d" % (name, self.uid), ap)

    def mark(self):
        return (self.a_off, self.b_off)

    def reset(self, mark):
        self.barrier()
        self.new_engine_sems()
        self.a_off, self.b_off = mark
        self.free_dsems.extend(self.stage_dsems)
        self.stage_dsems = []

    def _dsem(self, trk):
        if trk.dsem is None:
            if self.free_dsems:
                k = self.free_dsems.pop()
            else:
                k = ("D", self.n_dsem)
                self.sems[k] = self.es.enter_context(self.nc.semaphore("d%d" % self.n_dsem))
                self.n_dsem += 1
                self.dtot[k] = 0
            trk.dsem = k
            trk.dcnt = self.dtot[k]
            self.stage_dsems.append(k)
        return trk.dsem

    def ekey(self, eng):
        return ("E", eng) if self.ever == 0 else ("E", eng, self.ever)

    def op(self, eng, fn, R=(), W=(), pe_acc=False):
        R = [_trk(x) for x in R]
        W = [_trk(x) for x in W]
        waits = self._deps(eng, R, W)
        ek = self.ekey(eng)
        if pe_acc:
            waits = [(k, v) for (k, v) in waits if k != ek]
        self.cnt[eng] += 1
        tk = (ek, self.cnt[eng])
        self.ops[eng].append((waits, fn, (ek, 1)))
        for r in R:
            r.rs.append(tk)
        for w in W:
            w.w = tk
            w.rs = []
        return tk

    def new_engine_sems(self):
        self.ever += 1
        for e in ENGINES:
            self.sems[self.ekey(e)] = self.es.enter_context(self.nc.semaphore("s_%s_%d" % (e, self.ever)))
            self.cnt[e] = 0

    def barrier(self):
        tot = {}
        for e in ENGINES:
            if self.cnt[e] > 0:
                tot[self.ekey(e)] = self.cnt[e]
        for k, v in self.dtot.items():
            if v > 0:
                tot[k] = v
        if self.cc_cnt > 0:
            tot[("C", 0)] = self.cc_cnt
        for e in ENGINES:
            self.pending[e] = list(tot.items())

    def _deps(self, eng, R, W):
        waits = Prog._deps(self, eng, R, W)
        if self.pending[eng]:
            wd = self.waited[eng]
            for (k, v) in self.pending[eng]:
                if wd.get(k, 0) < v:
                    wd[k] = v
                    waits.append((k, v))
            self.pending[eng] = []
        return waits

    def dma(self, eng, out, in_, R, W, sb, **kw):
        tk = Prog.dma(self, eng, out, in_, R, W, sb, **kw)
        self.dtot[tk[0]] = tk[1]
        return tk

    def gather(self, dst_ap, dst_trk, table, spec, npart=128, bounds=None):
        col = len(self.specs)
        assert col < NIDX
        self.specs.append((spec, npart))
        dst_trk = _trk(dst_trk)
        waits = self._deps("gpsimd", [table.trk, self.idx_sb.trk], [dst_trk])
        k = self._dsem(dst_trk)
        dst_trk.dcnt += 16
        tk = (k, dst_trk.dcnt)
        self.dtot[k] = tk[1]
        idx_ap = self.idx_sb[0:npart, col:col + 1]
        tab = table.t
        if bounds is None:
            fn = lambda e: e.indirect_dma_start(out=dst_ap, out_offset=None, in_=tab, in_offset=bass.IndirectOffsetOnAxis(ap=idx_ap, axis=0))
        else:
            fn = lambda e: e.indirect_dma_start(out=dst_ap, out_offset=None, in_=tab, in_offset=bass.IndirectOffsetOnAxis(ap=idx_ap, axis=0),
                                                bounds_check=bounds, oob_is_err=False)
        self.ops["gpsimd"].append((waits, fn, (k, 16)))
        table.trk.rs.append(tk)
        dst_trk.w = tk
        dst_trk.rs = []
        return tk

    def all_gather(self, own, gat, extra_R=()):
        waits = self._deps("gpsimd", [own.trk] + list(extra_R), [gat.trk])
        self.cc_cnt += 1
        tk = (("C", 0), self.cc_cnt)
        oin, oout = own.t.bitcast(F32), gat.t.bitcast(F32)
        self.ops["gpsimd"].append((waits, lambda e: e.collective_compute("AllGather", ALU.bypass, replica_groups=[list(range(8))],
                                                                         ins=[oin], outs=[oout]), (("C", 0), 1)))
        own.trk.rs.append(tk)
        for t_ in extra_R:
            t_.rs.append(tk)
        gat.trk.w = tk
        gat.trk.rs = []
        return tk

    def idx_table(self, core):
        t = np.zeros((128, NIDX), np.int32)
        for col, (spec, npart) in enumerate(self.specs):
            v = np.asarray(spec(core), np.int64)
            assert v.shape == (npart,), (col, v.shape)
            t[0:npart, col] = v
        return t

    def build(self):
        fin = []
        for t in self.out_trks:
            if t.w is not None:
                fin.append(t.w)
        for e in ENGINES:
            if self.cnt[e] > 0 and e != "sync":
                fin.append((self.ekey(e), self.cnt[e]))
        fin.extend(self.pending["sync"])
        nc = self.nc
        ops = self.ops
        sems = self.sems
        with nc.Block() as block:
            def emit(engname):
                def body(e):
                    for (waits, fn, inc) in ops[engname]:
                        for (k, v) in waits:
                            e.wait_ge(sems[k], v)
                        ins = fn(e)
                        ins.then_inc(sems[inc[0]], inc[1])
                    if engname == "sync":
                        need = {}
                        for (k, v) in fin:
                            need[k] = max(need.get(k, 0), v)
                        for k, v in need.items():
                            e.wait_ge(sems[k], v)
                return body
            block.sync(emit("sync"))
            block.scalar(emit("scalar"))
            block.vector(emit("vector"))
            block.gpsimd(emit("gpsimd"))
            block.tensor(emit("tensor"))
        self.es.close()
        return nc


import math

D = 2048
NPROJ = 4672
OC = 6912
EPS = 1e-6
PI = math.pi


def rep_load(P, eng, dst, src_row_ap, n):
    P.dma(eng, dst[:], src_row_ap.partition_broadcast(128), R=[], W=[dst], sb=dst)


def stage_a(P, NT, dr, pay1, payP):
    NG = NT // 512
    x, pos, w_in = dr["x"], dr["pos"], dr["w_in"]
    g_mix, g_dq, g_dk, g_cq, g_ckv, g_mq, g_mk = dr["g_mix"], dr["g_dq"], dr["g_dk"], dr["g_cq"], dr["g_ckv"], dr["g_mq"], dr["g_mk"]
    w_uq, w_ukv, inv64, inv32 = dr["w_uq"], dr["w_ukv"], dr["inv64"], dr["inv32"]

    ident = P.sbuf("ident", [128, 128], BF16)
    P.op("gpsimd", lambda e: e.memset(ident[:], 1.0), W=[ident])
    P.op("gpsimd", lambda e: e.affine_select(out=ident[:], in_=ident[:], pattern=[[-1, 128]], compare_op=ALU.is_equal,
                                            fill=0.0, base=0, channel_multiplier=1), R=[ident], W=[ident])
    negpi = P.sbuf("negpi", [128, 1], F32)
    P.op("gpsimd", lambda e: e.memset(negpi[:], -PI), W=[negpi])
    gmix = P.sbuf("gmix", [128, D], F32)
    rep_load(P, "sync", gmix, g_mix[0:1, :], D)
    reps = {}
    for nm, src, n in (("gdq", g_dq, 128), ("gdk", g_dk, 128), ("gcq", g_cq, 512), ("gckv", g_ckv, 256),
                       ("gmq", g_mq, 192), ("gmk", g_mk, 192), ("inv64", inv64, 64), ("inv32", inv32, 32)):
        t = P.sbuf("r_" + nm, [128, n], F32)
        rep_load(P, "sync", t, src[0:1, :], n)
        reps[nm] = t
    wuq = P.sbuf("wuq", [128, 4, 1152], BF16)
    P.dma("gpsimd", wuq[:], w_uq[:].rearrange("(k p) c -> p k c", p=128), R=[], W=[wuq], sb=wuq)
    wukv = P.sbuf("wukv", [128, 2, 1536], BF16)
    P.dma("gpsimd", wukv[:], w_ukv[:].rearrange("(k p) c -> p k c", p=128), R=[], W=[wukv], sb=wukv)

    xt = [P.sbuf("xt%d" % i, [128, D], F32) for i in range(1)]
    ub = P.sbuf("ub", [128, D], BF16)
    sq = P.sbuf("sq", [128, 1152], F32)
    kfb = P.sbuf("kfb", [128, 1152], F32)
    SBo = P.sbuf("SBo", [128, 4, 1536], BF16)
    st = P.sbuf("st", [128, 16], F32)
    uT = P.sbuf("uT", [128, 16, 512], BF16)
    wsl = [P.sbuf("wsl%d" % i, [128, 16, 512], BF16) for i in range(2)]
    Pt = [P.sbuf("Pt%d" % i, [128, NPROJ - 1536], F32) for i in range(4)]
    OT = [P.sbuf("OT%d" % i, [128, OC - 1536], BF16) for i in range(1)]
    psT = P.psum("psT", [128, 1024], BF16)
    psO = P.psum("psO", [128, 1024], BF16)
    tps = [P.sbuf("tps%d" % i, [128, 8, 128], BF16) for i in range(2)]
    ocnt = {"n": 0}
    psP = [P.psum("psP%d" % i, [128, 512], F32) for i in range(2)]
    psM = [P.psum("psM%d" % i, [128, 512], F32) for i in range(3)]
    cs64 = P.sbuf("cs64", [128, 2, 64], F32)
    cs32 = P.sbuf("cs32", [128, 2, 32], F32)
    posi = P.sbuf("posi", [128, 1], I32)
    posf = P.sbuf("posf", [128, 1], F32)
    ang = P.sbuf("ang", [128, 128], F32)
    angk = P.sbuf("angk", [128, 128], F32)
    angi = P.sbuf("angi", [128, 128], I32)
    hs = P.sbuf("hs", [128, 1536], F32)
    hs2 = P.sbuf("hs2", [128, 1536], F32)
    cb = P.sbuf("cb", [128, 512], BF16)
    cT = P.sbuf("cT", [128, 4, 128], BF16)
    qf = P.sbuf("qf", [128, 1536], F32)

    V = "vector"
    G = "gpsimd"
    S = "scalar"
    T = "tensor"

    def rstd_from_ss(ss_ap, n, trk):
        P.op(V, lambda e: e.tensor_scalar(out=ss_ap, in0=ss_ap, scalar1=1.0 / n, scalar2=EPS, op0=ALU.mult, op1=ALU.add), R=[trk], W=[trk])
        P.op(S, lambda e: e.activation(out=ss_ap, in_=ss_ap, func=AF.Sqrt), R=[trk], W=[trk])
        P.op(V, lambda e: e.reciprocal(out=ss_ap, in_=ss_ap), R=[trk], W=[trk])

    def headnorm(src, H, dh, gain, dst, Rt, Wt):
        sqv = sq[:, 0:H * dh].rearrange("p (h c) -> p h c", h=H)
        P.op(V, lambda e: e.tensor_tensor(out=sqv, in0=src, in1=src, op=ALU.mult), R=Rt, W=[sq])
        P.op(V, lambda e: e.reduce_sum(out=st[:, 0:H], in_=sqv, axis=AX.X), R=[sq], W=[st])
        rstd_from_ss(st[:, 0:H], dh, st)
        P.op(V, lambda e: e.tensor_tensor(out=dst, in0=src, in1=st[:, 0:H].unsqueeze(2).to_broadcast([128, H, dh]), op=ALU.mult), R=Rt + [st], W=Wt)
        P.op(G, lambda e: e.tensor_tensor(out=dst, in0=dst, in1=gain[:, 0:dh].unsqueeze(1).to_broadcast([128, H, dh]), op=ALU.mult), R=Wt + [gain], W=Wt)

    def rope(src, H, half, cs, dst, Rt, Wt):
        x1 = src[:, :, 0:half]
        x2 = src[:, :, half:2 * half]
        cosb = cs[:, 0, :].unsqueeze(1).to_broadcast([128, H, half])
        sinb = cs[:, 1, :].unsqueeze(1).to_broadcast([128, H, half])
        t = hs2[:, 0:4 * H * half].rearrange("p (a h c) -> p a h c", a=4, h=H)
        P.op(V, lambda e: e.tensor_tensor(out=t[:, 0], in0=x1, in1=cosb, op=ALU.mult), R=Rt + [cs], W=[hs2.sub(0)])
        P.op(G, lambda e: e.tensor_tensor(out=t[:, 1], in0=x2, in1=sinb, op=ALU.mult), R=Rt + [cs], W=[hs2.sub(1)])
        P.op(V, lambda e: e.tensor_tensor(out=t[:, 2], in0=x2, in1=cosb, op=ALU.mult), R=Rt + [cs], W=[hs2.sub(2)])
        P.op(G, lambda e: e.tensor_tensor(out=t[:, 3], in0=x1, in1=sinb, op=ALU.mult), R=Rt + [cs], W=[hs2.sub(3)])
        P.op(V, lambda e: e.tensor_tensor(out=dst[:, :, 0:half], in0=t[:, 0], in1=t[:, 1], op=ALU.subtract), R=[hs2.sub(0), hs2.sub(1)], W=Wt)
        P.op(G, lambda e: e.tensor_tensor(out=dst[:, :, half:2 * half], in0=t[:, 2], in1=t[:, 3], op=ALU.add), R=[hs2.sub(2), hs2.sub(3)], W=Wt)

    def sincos(cs, inv, half):
        a = ang[:, 0:2 * half].rearrange("p (a c) -> p a c", a=2)
        kf = angk[:, 0:2 * half].rearrange("p (a c) -> p a c", a=2)
        ki = angi[:, 0:2 * half].rearrange("p (a c) -> p a c", a=2)
        P.op(V, lambda e: e.tensor_scalar(out=a[:, 1, :], in0=inv[:, 0:half], scalar1=posf[:, 0:1], scalar2=None, op0=ALU.mult), R=[inv, posf], W=[ang])
        P.op(V, lambda e: e.tensor_scalar(out=a[:, 0, :], in0=a[:, 1, :], scalar1=0.5 * PI, scalar2=None, op0=ALU.add), R=[ang], W=[ang])
        P.op(V, lambda e: e.tensor_scalar(out=ki, in0=a, scalar1=1.0 / (2 * PI), scalar2=None, op0=ALU.mult), R=[ang], W=[angi])
        P.op(V, lambda e: e.tensor_copy(out=kf, in_=ki), R=[angi], W=[angk])
        HI = 6.28125
        LO = 2 * PI - HI
        P.op(V, lambda e: e.scalar_tensor_tensor(out=a, in0=kf, scalar=-HI, in1=a, op0=ALU.mult, op1=ALU.add), R=[angk, ang], W=[ang])
        P.op(V, lambda e: e.scalar_tensor_tensor(out=a, in0=kf, scalar=-LO, in1=a, op0=ALU.mult, op1=ALU.add), R=[angk, ang], W=[ang])
        P.op(V, lambda e: e.tensor_scalar(out=kf, in0=a, scalar1=PI, scalar2=-2 * PI, op0=ALU.is_gt, op1=ALU.mult), R=[ang], W=[angk])
        P.op(V, lambda e: e.tensor_tensor(out=a, in0=a, in1=kf, op=ALU.add), R=[ang, angk], W=[ang])
        P.op(V, lambda e: e.tensor_scalar(out=kf, in0=a, scalar1=-PI, scalar2=2 * PI, op0=ALU.is_lt, op1=ALU.mult), R=[ang], W=[angk])
        P.op(V, lambda e: e.tensor_tensor(out=a, in0=a, in1=kf, op=ALU.add), R=[ang, angk], W=[ang])
        P.op(V, lambda e: e.tensor_scalar(out=a, in0=a, scalar1=-PI, scalar2=PI, op0=ALU.max, op1=ALU.min), R=[ang], W=[ang])
        P.op(S, lambda e: e.activation(out=cs[:], in_=a, func=AF.Sin), R=[ang], W=[cs])

    wq = ["sync", "gpsimd"]
    nslab = (NPROJ + 511) // 512
    tile_no = 0
    for g in range(NG):
        for tt in range(4):
            r0 = (g * 4 + tt) * 128
            xb_ = xt[0]
            tile_no += 1
            P.dma("sync", xb_[:], x[r0:r0 + 128, :], R=[], W=[xb_], sb=xb_)
            P.op(S, lambda e, xb_=xb_: e.activation(out=ub[:], in_=xb_[:], func=AF.Square, accum_out=st[:, 0:1]), R=[xb_], W=[ub, st])
            rstd_from_ss(st[:, 0:1], D, st)
            P.op(V, lambda e, xb_=xb_: e.scalar_tensor_tensor(out=ub[:], in0=xb_[:], scalar=st[:, 0:1], in1=gmix[:], op0=ALU.mult, op1=ALU.mult), R=[xb_, st, gmix], W=[ub])
            for half in range(2):
                for j in range(8):
                    c = half * 8 + j
                    P.op(T, lambda e, c=c, j=j: e.transpose(psT[:, j * 128:(j + 1) * 128], ub[:, c * 128:(c + 1) * 128], ident[:]), R=[ub, ident], W=[psT])
                eng = S if half == 0 else V
                if eng == S:
                    P.op(S, lambda e, half=half, tt=tt: e.activation(out=uT[:, half * 8:(half + 1) * 8, tt * 128:(tt + 1) * 128],
                                                                   in_=psT[:].rearrange("p (j t) -> p j t", j=8), func=AF.Copy), R=[psT], W=[uT.sub(tt)])
                else:
                    P.op(V, lambda e, half=half, tt=tt: e.tensor_copy(out=uT[:, half * 8:(half + 1) * 8, tt * 128:(tt + 1) * 128],
                                                                    in_=psT[:].rearrange("p (j t) -> p j t", j=8)), R=[psT], W=[uT.sub(tt)])
        for s in range(nslab):
            c0 = s * 512
            cw = min(512, NPROJ - c0)
            wb = wsl[s % 2]
            P.dma("gpsimd", wb[:, :, 0:cw], w_in[:, c0:c0 + cw].rearrange("(k p) c -> p k c", p=128), R=[], W=[wb], sb=wb)
            for tt in range(4):
                pp = psP[(s * 4 + tt) % 2]
                for k in range(16):
                    P.op(T, lambda e, pp=pp, k=k, tt=tt, wb=wb, cw=cw: e.matmul(pp[:, 0:cw], lhsT=uT[:, k, tt * 128:(tt + 1) * 128], rhs=wb[:, k, 0:cw],
                                                                               start=(k == 0), stop=(k == 15)),
                         R=[uT.sub(tt), wb], W=[pp], pe_acc=(k > 0))
                if s < 3:
                    dst = SBo[:, tt, c0:c0 + cw]
                    dtr = SBo.sub((tt, s))
                else:
                    dst = Pt[tt][:, c0 - 1536:c0 - 1536 + cw]
                    dtr = Pt[tt].sub(s)
                if tt % 2 == 0:
                    P.op(S, lambda e, pp=pp, dst=dst, cw=cw: e.activation(out=dst, in_=pp[:, 0:cw], func=AF.Copy), R=[pp], W=[dtr])
                else:
                    P.op(V, lambda e, pp=pp, dst=dst, cw=cw: e.tensor_copy(out=dst, in_=pp[:, 0:cw]), R=[pp], W=[dtr])
        for tt in range(4):
            r0 = (g * 4 + tt) * 128
            Pb = Pt[tt]
            PR = [Pb.sub(s) for s in range(3, nslab)]
            ot = OT[0]
            P.dma("sync", posi[:], pos[r0:r0 + 128, :], R=[], W=[posi], sb=posi)
            P.op(V, lambda e: e.tensor_copy(out=posf[:], in_=posi[:]), R=[posi], W=[posf])
            sincos(cs64, reps["inv64"], 64)
            sincos(cs32, reps["inv32"], 32)
            P.op(S, lambda e, Pb=Pb, ot=ot: e.activation(out=ot[:, 3072 - 1536:3840 - 1536], in_=Pb[:, 3072 - 1536:3840 - 1536], func=AF.Copy), R=PR, W=[ot.sub("b")])
            for (c0, gn, oc0, key) in ((1536, "gdq", 1536, "c"), (2304, "gdk", 2304, "d")):
                src = Pb[:, c0 - 1536:c0 - 1536 + 768].rearrange("p (h c) -> p h c", h=6)
                tmp = hs[:, 0:768].rearrange("p (h c) -> p h c", h=6)
                headnorm(src, 6, 128, reps[gn], tmp, PR, [hs])
                rope(tmp, 6, 64, cs64, ot[:, oc0 - 1536:oc0 - 1536 + 768].rearrange("p (h c) -> p h c", h=6), [hs], [ot.sub(key)])
            cq = Pb[:, 3840 - 1536:4352 - 1536]
            P.op(S, lambda e, cq=cq: e.activation(out=sq[:, 0:512], in_=cq, func=AF.Square, accum_out=st[:, 0:1]), R=PR, W=[sq, st])
            rstd_from_ss(st[:, 0:1], 512, st)
            P.op(V, lambda e, cq=cq: e.scalar_tensor_tensor(out=cb[:, 0:512], in0=cq, scalar=st[:, 0:1], in1=reps["gcq"][:], op0=ALU.mult, op1=ALU.mult), R=PR + [st, reps["gcq"]], W=[cb])
            for j in range(4):
                P.op(T, lambda e, j=j: e.transpose(psT[:, j * 128:(j + 1) * 128], cb[:, j * 128:(j + 1) * 128], ident[:]), R=[cb, ident], W=[psT])
            P.op(V, lambda e: e.tensor_copy(out=cT[:], in_=psT[:, 0:512].rearrange("p (j t) -> p j t", j=4)), R=[psT], W=[cT])
            for (n0, nw, pi) in ((0, 512, 0), (512, 512, 1), (1024, 128, 2)):
                for k in range(4):
                    P.op(T, lambda e, k=k, n0=n0, nw=nw, pi=pi: e.matmul(psM[pi][:, 0:nw], lhsT=cT[:, k, :], rhs=wuq[:, k, n0:n0 + nw], start=(k == 0), stop=(k == 3)),
                         R=[cT, wuq], W=[psM[pi]], pe_acc=(k > 0))
                P.op(S, lambda e, n0=n0, nw=nw, pi=pi: e.activation(out=qf[:, n0:n0 + nw], in_=psM[pi][:, 0:nw], func=AF.Copy), R=[psM[pi]], W=[qf.sub(pi)])
            qv = qf[:, 0:1152].rearrange("p (h c) -> p h c", h=6)
            qn = hs[:, 0:1152].rearrange("p (h c) -> p h c", h=6)
            headnorm(qv, 6, 192, reps["gmq"], qn, [qf.sub(0), qf.sub(1), qf.sub(2)], [hs])
            oq = ot[:, 3840 - 1536:4992 - 1536].rearrange("p (h c) -> p h c", h=6)
            P.op(S, lambda e, oq=oq, qn=qn: e.activation(out=oq[:, :, 0:128], in_=qn[:, :, 0:128], func=AF.Copy), R=[hs], W=[ot.sub("e")])
            rope(qn[:, :, 128:192], 6, 32, cs32, oq[:, :, 128:192], [hs], [ot.sub("f")])
            ckv = Pb[:, 4352 - 1536:4608 - 1536]
            P.op(S, lambda e, ckv=ckv: e.activation(out=sq[:, 0:256], in_=ckv, func=AF.Square, accum_out=st[:, 0:1]), R=PR, W=[sq, st])
            rstd_from_ss(st[:, 0:1], 256, st)
            P.op(V, lambda e, ckv=ckv: e.scalar_tensor_tensor(out=cb[:, 0:256], in0=ckv, scalar=st[:, 0:1], in1=reps["gckv"][:], op0=ALU.mult, op1=ALU.mult), R=PR + [st, reps["gckv"]], W=[cb])
            for j in range(2):
                P.op(T, lambda e, j=j: e.transpose(psT[:, j * 128:(j + 1) * 128], cb[:, j * 128:(j + 1) * 128], ident[:]), R=[cb, ident], W=[psT])
            P.op(V, lambda e: e.tensor_copy(out=cT[:, 0:2, :], in_=psT[:, 0:256].rearrange("p (j t) -> p j t", j=2)), R=[psT], W=[cT])
            for pi in range(3):
                n0 = pi * 512
                for k in range(2):
                    P.op(T, lambda e, k=k, n0=n0, pi=pi: e.matmul(psM[pi][:, 0:512], lhsT=cT[:, k, :], rhs=wukv[:, k, n0:n0 + 512], start=(k == 0), stop=(k == 1)),
                         R=[cT, wukv], W=[psM[pi]], pe_acc=(k > 0))
                P.op(S, lambda e, n0=n0, pi=pi: e.activation(out=qf[:, n0:n0 + 512], in_=psM[pi][:, 0:512], func=AF.Copy), R=[psM[pi]], W=[qf.sub(pi)])
            kvv = qf[:, 0:1536].rearrange("p (h c) -> p h c", h=6)
            QR = [qf.sub(0), qf.sub(1), qf.sub(2)]
            ov = ot[:, 6144 - 1536:6912 - 1536].rearrange("p (h c) -> p h c", h=6)
            P.op(S, lambda e, ov=ov, kvv=kvv: e.activation(out=ov, in_=kvv[:, :, 128:256], func=AF.Copy), R=QR, W=[ot.sub("g")])
            kf = kfb[:, 0:1152].rearrange("p (h c) -> p h c", h=6)
            P.op(V, lambda e, kf=kf, kvv=kvv: e.tensor_copy(out=kf[:, :, 0:128], in_=kvv[:, :, 0:128]), R=QR, W=[kfb.sub(0)])
            P.op(G, lambda e, kf=kf, Pb=Pb: e.tensor_copy(out=kf[:, :, 128:192], in_=Pb[:, 4608 - 1536:4672 - 1536].unsqueeze(1).to_broadcast([128, 6, 64])), R=PR, W=[kfb.sub(1)])
            kn = hs[:, 0:1152].rearrange("p (h c) -> p h c", h=6)
            headnorm(kf, 6, 192, reps["gmk"], kn, [kfb.sub(0), kfb.sub(1)], [hs])
            ok_ = ot[:, 4992 - 1536:6144 - 1536].rearrange("p (h c) -> p h c", h=6)
            P.op(S, lambda e, ok_=ok_, kn=kn: e.activation(out=ok_[:, :, 0:128], in_=kn[:, :, 0:128], func=AF.Copy), R=[hs], W=[ot.sub("h")])
            rope(kn[:, :, 128:192], 6, 32, cs32, ok_[:, :, 128:192], [hs], [ot.sub("i")])
            OTR = [ot.sub(k) for k in "bcdefghi"]
            SBR = [SBo.sub((tt, 0)), SBo.sub((tt, 1)), SBo.sub((tt, 2))]
            for c8 in range(0, 54, 8):
                n8 = min(8, 54 - c8)
                tb = tps[ocnt["n"] % 2]
                ocnt["n"] += 1
                for j in range(n8):
                    ch = c8 + j
                    if ch < 12:
                        src, RR = SBo[:, tt, ch * 128:(ch + 1) * 128], SBR
                    else:
                        src, RR = ot[:, (ch - 12) * 128:(ch - 11) * 128], OTR
                    P.op(T, lambda e, j=j, src=src: e.transpose(psO[:, j * 128:(j + 1) * 128], src, ident[:]), R=RR + [ident], W=[psO])
                if (c8 // 8) % 2 == 0:
                    P.op(V, lambda e, tb=tb, n8=n8: e.tensor_copy(out=tb[:, 0:n8, :], in_=psO[:, 0:n8 * 128].rearrange("p (j t) -> p j t", j=n8)), R=[psO], W=[tb])
                else:
                    P.op(S, lambda e, tb=tb, n8=n8: e.activation(out=tb[:, 0:n8, :], in_=psO[:, 0:n8 * 128].rearrange("p (j t) -> p j t", j=n8), func=AF.Copy), R=[psO], W=[tb])
                P.dma("sync", pay1[c8 * 128:(c8 + n8) * 128, r0:r0 + 128].rearrange("(j p) t -> p j t", p=128), tb[:, 0:n8, :], R=[tb], W=[pay1.sub((r0, c8))], sb=tb)
                lo, hi = max(c8, 30), min(c8 + n8, 39)
                if lo < hi:
                    q = r0 // 512
                    colp = (q % 2) * (NT // 2) + (q // 2) * 512 + (r0 % 512)
                    P.dma("sync", payP[(lo - 30) * 128:(hi - 30) * 128, colp:colp + 128].rearrange("(j p) t -> p j t", p=128), tb[:, lo - c8:hi - c8, :], R=[tb], W=[payP.sub((r0, c8))], sb=tb)


import math

V_, G_, S_, T_ = "vector", "gpsimd", "scalar", "tensor"


ROWS1 = 8064
ROWS2 = 2304
F_SBQ, F_SBK, F_SBV, F_DQ, F_DK, F_DV, F_MQ, F_MK, F_MV, F_MQP = 0, 512, 1024, 1536, 2304, 3072, 3840, 4992, 6144, 6912


def g1row(core, f, hh):
    return (core * ROWS1 + f) * 2 + hh


def stage_b(P, S, dr, G1v, own2, dbg=None):
    NQ = S // 512
    NB = S // 128
    HS = S // 2
    NDB = HS // 128
    RK = S // 4
    msd, md, mka, mkb, dmask = dr["msd"], dr["md"], dr["mka"], dr["mkb"], dr["dmask"]
    pr = np.arange(128)

    def load_rows(dst, dst_trk, g, ranks, fspec, npart=128, parity=None):
        for slot in range(len(ranks)):
            if parity is None:
                for hh in range(2):
                    col = slot * RK + hh * 1024
                    P.gather(dst[0:npart, g, col:col + 1024], dst_trk, G1v,
                             (lambda c, slot=slot, hh=hh: g1row(ranks[slot](c), fspec(c) + pr[:npart], hh)), npart=npart)
            else:
                col = slot * 1024
                P.gather(dst[0:npart, g, col:col + 1024], dst_trk, G1v,
                         (lambda c, slot=slot: g1row(ranks[slot](c), fspec(c) + pr[:npart], parity(c))), npart=npart)

    batch_ranks = [(lambda c, r=r: 4 * (c // 4) + r) for r in range(4)]

    qT = P.sbuf("qT", [128, 2, S], BF16)
    kT = P.sbuf("kT", [128, 2, S], BF16)
    nkT = P.sbuf("nkT", [128, S], BF16)
    vv = P.sbuf("vv", [128, NB, 128], BF16)
    vT = P.sbuf("vT", [128, 1, S], BF16)
    ident = P.sbuf("identb", [128, 128], BF16)
    P.op(G_, lambda e: e.memset(ident[:], 1.0), W=[ident])
    P.op(G_, lambda e: e.affine_select(out=ident[:], in_=ident[:], pattern=[[-1, 128]], compare_op=ALU.is_equal, fill=0.0, base=0, channel_multiplier=1), R=[ident], W=[ident])
    msk = [P.sbuf("msk%d" % i, [128, 4, 512], BF16) for i in range(2)]
    ones = P.sbuf("ones", [128, 128], BF16)
    tri = P.sbuf("tri", [128, 128], BF16)
    P.op(G_, lambda e: e.memset(ones[:], 1.0), W=[ones])
    one_c = P.sbuf("one_c", [128, 1], F32)
    P.op(G_, lambda e: e.memset(one_c[:], 1.0), W=[one_c])
    P.op(G_, lambda e: e.memset(tri[:], 1.0), W=[tri])
    P.op(G_, lambda e: e.affine_select(out=tri[:], in_=tri[:], pattern=[[-1, 128]], compare_op=ALU.is_ge, fill=0.0, base=0, channel_multiplier=1), R=[tri], W=[tri])
    e1 = [P.sbuf("e1_%d" % i, [128, 512], F32) for i in range(2)]
    Lb = [P.sbuf("Lb%d" % i, [128, 512], BF16) for i in range(2)]
    Rf = P.sbuf("Rf", [128, 512], F32)
    Rb = [P.sbuf("Rb%d" % i, [128, 512], BF16) for i in range(2)]
    pT = [P.sbuf("pT%d" % i, [128, 512], BF16) for i in range(3)]
    rden = P.sbuf("rden", [128, 512], F32)
    ob = [P.sbuf("ob%d" % i, [128, 512], BF16) for i in range(2)]
    psS = [P.psum("psS%d" % i, [128, 512], F32) for i in range(2)]
    psC = [P.psum("psC%d" % i, [128, 512], F32) for i in range(2)]
    psN = [P.psum("psN%d" % i, [128, 512], F32) for i in range(2)]
    psD = [P.psum("psD%d" % i, [128, 512], F32) for i in range(2)]
    psVT = [Buf("psVT%d" % i, P.banks[2 + i][:].bitcast(BF16)) for i in range(2)]
    for i in range(2):
        psVT[i].trk = psC[i].trk
    cnt = {"t": 0, "c": 0}

    def v_transposes(nblk, cols_of, src=None, src_trk=None):
        src = vT[:, 0, :] if src is None else src
        src_trk = vT if src_trk is None else src_trk
        for i0_ in range(0, nblk, 8):
            n8 = min(8, nblk - i0_)
            pb = psVT[(i0_ // 8) % 2]
            for j in range(n8):
                P.op(T_, lambda e, j=j, pb=pb, sl=cols_of(i0_ + j): e.transpose(pb[:, j * 128:(j + 1) * 128], src[:, sl], ident[:]), R=[src_trk, ident], W=[pb])
            if (i0_ // 8) % 2 == 0:
                P.op(V_, lambda e, pb=pb, i0_=i0_, n8=n8: e.tensor_copy(out=vv[:, i0_:i0_ + n8, :], in_=pb[:, 0:n8 * 128].rearrange("p (j t) -> p j t", j=n8)), R=[pb], W=[vv])
            else:
                P.op(S_, lambda e, pb=pb, i0_=i0_, n8=n8: e.activation(out=vv[:, i0_:i0_ + n8, :], in_=pb[:, 0:n8 * 128].rearrange("p (j t) -> p j t", j=n8), func=AF.Copy), R=[pb], W=[vv])

    if True:
        scale = 128 ** -0.5
        sbf = lambda base: (lambda c: base + (c % 4) * 128)
        load_rows(qT, qT, 0, batch_ranks, sbf(F_SBQ))
        load_rows(kT, kT, 0, batch_ranks, sbf(F_SBK))
        load_rows(vT, vT, 0, batch_ranks, sbf(F_SBV))
        v_transposes(NB, lambda i: slice(i * 128, (i + 1) * 128))
        P.dma("sync", msk[0][:], msd[:], R=[], W=[msk[0]], sb=msk[0])
        P.op(S_, lambda e: e.activation(out=kT[:, 0, :], in_=kT[:, 0, :], func=AF.Copy, scale=scale), R=[kT], W=[kT])
        P.op(V_, lambda e: e.tensor_scalar(out=nkT[:], in0=kT[:, 0, :], scalar1=-1.0, scalar2=None, op0=ALU.mult), R=[kT], W=[nkT])
        for c in range(NQ):
            qs = slice(c * 512, (c + 1) * 512)
            pn = psN[c % 2]
            blocks = []
            for kt in (3, 2, 1, 0):
                blocks.append((c * 4 + kt, kt))
            for kb in range(c * 4 - 1, -1, -1):
                blocks.append((kb, None))
            nblk = len(blocks)
            for bi, (kb, mk) in enumerate(blocks):
                i = cnt["t"]
                cnt["t"] += 1
                ps, pc = psS[i % 2], psC[i % 2]
                ks = slice(kb * 128, (kb + 1) * 128)
                ee, lb, rb, pt = e1[i % 2], Lb[i % 2], Rb[i % 2], pT[i % 3]
                P.op(T_, lambda e, ps=ps, ks=ks, qs=qs: e.matmul(ps[:], lhsT=kT[:, 0, ks], rhs=qT[:, 0, qs], start=True, stop=True), R=[kT, qT], W=[ps])
                P.op(S_, lambda e, ee=ee, ps=ps: e.activation(out=ee[:], in_=ps[:], func=AF.Exp), R=[ps], W=[ee])
                P.op(S_, lambda e, ee=ee, lb=lb: e.activation(out=lb[:], in_=ee[:], func=AF.Ln, bias=one_c[:, 0:1], scale=1.0), R=[ee, one_c], W=[lb])
                if mk is not None:
                    P.op(V_, lambda e, lb=lb, mk=mk: e.tensor_tensor(out=lb[:], in0=lb[:], in1=msk[0][:, mk, :], op=ALU.mult), R=[lb, msk[0]], W=[lb])
                first = (bi == 0)
                P.op(T_, lambda e, pc=pc, ks=ks, qs=qs: e.matmul(pc[:], lhsT=nkT[:, ks], rhs=qT[:, 0, qs], start=True, stop=False), R=[nkT, qT], W=[pc])
                P.op(T_, lambda e, pc=pc, lb=lb, first=first: e.matmul(pc[:], lhsT=tri[:], rhs=lb[:], start=False, stop=first), R=[tri, lb], W=[pc], pe_acc=True)
                if not first:
                    P.op(T_, lambda e, pc=pc, rb=rb: e.matmul(pc[:], lhsT=ones[:], rhs=rb[:], start=False, stop=True), R=[ones, rb], W=[pc], pe_acc=True)
                P.op(S_, lambda e, pt=pt, pc=pc: e.activation(out=pt[:], in_=pc[:], func=AF.Exp, scale=-1.0), R=[pc], W=[pt])
                if mk is not None:
                    P.op(V_, lambda e, pt=pt, mk=mk: e.tensor_tensor(out=pt[:], in0=pt[:], in1=msk[0][:, mk, :], op=ALU.mult), R=[pt, msk[0]], W=[pt])
                P.op(T_, lambda e, pn=pn, kb=kb, pt=pt, bi=bi, nblk=nblk: e.matmul(pn[:], lhsT=vv[:, kb, :], rhs=pt[:], start=(bi == 0), stop=(bi == nblk - 1)),
                     R=[vv, pt], W=[pn], pe_acc=(bi > 0))
                if bi < nblk - 1:
                    nrb = Rb[(i + 1) % 2]
                    if first:
                        P.op(G_, lambda e, lb=lb: e.tensor_copy(out=Rf[:], in_=lb[:]), R=[lb], W=[Rf])
                    else:
                        P.op(G_, lambda e, lb=lb: e.tensor_tensor(out=Rf[:], in0=Rf[:], in1=lb[:], op=ALU.add), R=[lb, Rf], W=[Rf])
                    P.op(G_, lambda e, nrb=nrb: e.tensor_copy(out=nrb[:], in_=Rf[:]), R=[Rf], W=[nrb])
            o = ob[c % 2]
            P.op(V_, lambda e, o=o, pn=pn: e.tensor_copy(out=o[:], in_=pn[:]), R=[pn], W=[o])
            P.dma("sync", own2[(c // 4) * 128:(c // 4 + 1) * 128, (c % 4) * 512:(c % 4 + 1) * 512], o[:], R=[o], W=[own2.sub(("sb", c))], sb=o)

    def softmax_unit(unit_of, parity, nslots, slot_blocks, out_pos, scale, mask_loads):
        nq = nslots * 512
        ub_ = lambda c: unit_of(c) // 6
        uh_ = lambda c: unit_of(c) % 6
        uranks = [(lambda c, r=r: 4 * ub_(c) + r) for r in range(4)]
        if parity is None:
            load_rows(qT, qT, 0, uranks, lambda c: F_MQ + uh_(c) * 192)
            load_rows(qT, qT, 1, uranks, lambda c: F_MQ + uh_(c) * 192 + 128, npart=64)
        else:
            load_rows(qT, qT, 0, uranks, lambda c: F_MQP + uh_(c) * 192, parity=parity)
            load_rows(qT, qT, 1, uranks, lambda c: F_MQP + uh_(c) * 192 + 128, npart=64, parity=parity)
        load_rows(kT, kT, 0, uranks, lambda c: F_MK + uh_(c) * 192)
        load_rows(kT, kT, 1, uranks, lambda c: F_MK + uh_(c) * 192 + 128, npart=64)
        load_rows(vT, vT, 0, uranks, lambda c: F_MV + uh_(c) * 128)
        v_transposes(NB, lambda i: slice(i * 128, (i + 1) * 128))
        for (mi, src) in mask_loads:
            P.dma("sync", msk[mi][:], src[:], R=[], W=[msk[mi]], sb=msk[mi])
        for j in range(nslots):
            qs = slice(j * 512, (j + 1) * 512)
            pn, pd = psN[j % 2], psD[j % 2]
            tiles = []
            for (kblk, mi) in slot_blocks(j):
                for kt in range(4):
                    tiles.append((kblk * 4 + kt, mi, kt))
            nt = len(tiles)
            for bi, (kb, mi, kt) in enumerate(tiles):
                i = cnt["t"]
                cnt["t"] += 1
                ps = psS[i % 2]
                pt = pT[i % 3]
                ks = slice(kb * 128, (kb + 1) * 128)
                P.op(T_, lambda e, ps=ps, ks=ks, qs=qs: e.matmul(ps[:], lhsT=kT[:, 0, ks], rhs=qT[:, 0, qs], start=True, stop=False), R=[kT, qT], W=[ps])
                P.op(T_, lambda e, ps=ps, ks=ks, qs=qs: e.matmul(ps[:], lhsT=kT[0:64, 1, ks], rhs=qT[0:64, 1, qs], start=False, stop=True), R=[kT, qT], W=[ps], pe_acc=True)
                P.op(S_, lambda e, pt=pt, ps=ps: e.activation(out=pt[:], in_=ps[:], func=AF.Exp, scale=scale), R=[ps], W=[pt])
                if mi is not None:
                    eng = V_ if (i % 2 == 0) else G_
                    P.op(eng, lambda e, pt=pt, mi=mi, kt=kt: e.tensor_tensor(out=pt[:], in0=pt[:], in1=msk[mi][:, kt, :], op=ALU.mult), R=[pt, msk[mi]], W=[pt])
                P.op(T_, lambda e, pn=pn, kb=kb, pt=pt, bi=bi, nt=nt: e.matmul(pn[:], lhsT=vv[:, kb, :], rhs=pt[:], start=(bi == 0), stop=(bi == nt - 1)), R=[vv, pt], W=[pn], pe_acc=(bi > 0))
                P.op(T_, lambda e, pd=pd, pt=pt, bi=bi, nt=nt: e.matmul(pd[:], lhsT=ones[:], rhs=pt[:], start=(bi == 0), stop=(bi == nt - 1)), R=[ones, pt], W=[pd], pe_acc=(bi > 0))
            o = ob[j % 2]
            P.op(V_, lambda e, pd=pd: e.reciprocal(out=rden[:], in_=pd[:]), R=[pd], W=[rden])
            P.op(V_, lambda e, o=o, pn=pn: e.tensor_tensor(out=o[:], in0=pn[:], in1=rden[:], op=ALU.mult), R=[pn, rden], W=[o])
            grp, col = out_pos(j)
            P.dma("sync", own2[grp * 128:(grp + 1) * 128, col:col + 512], o[:], R=[o], W=[own2.sub((grp, col))], sb=o)

    mscale = 192 ** -0.5
    for s3 in range(3):
        softmax_unit((lambda c, s3=s3: (c + 8 * s3) // 2), (lambda c: c % 2), NQ // 2,
                     lambda j: [(b_, None) for b_ in range(2 * j)] + [(2 * j, 0), (2 * j + 1, 1)],
                     (lambda j, s3=s3: (4 + 4 * s3 + j // 2, (j % 2) * 512)), mscale, [(0, mka), (1, mkb)])

    if True:
        dscale = 128 ** -0.5
        dm = P.sbuf("dm", [128, 3, 128], BF16)
        P.dma("sync", dm[:], dmask[:], R=[], W=[dm], sb=dm)
        numS = P.sbuf("numS", [128, HS], F32)
        denS = P.sbuf("denS", [128, HS], F32)
        db_ = lambda c: c // 4
        dh_ = lambda c: (c // 2) % 2
        dhf_ = lambda c: c % 2
        qranks = [(lambda c, r=r: 4 * db_(c) + 2 * dhf_(c) + r) for r in range(2)]
        kranks = [(lambda c, r=r: 4 * db_(c) + max(0, 2 * dhf_(c) - 1 + r)) for r in range(3)]
        for g, d in enumerate((1, 4, 16)):
            load_rows(qT, qT, 0, qranks, lambda c, g=g: F_DQ + (2 * g + dh_(c)) * 128)
            load_rows(kT, kT, 0, kranks, lambda c, g=g: F_DK + (2 * g + dh_(c)) * 128)
            load_rows(vT, vT, 0, kranks, lambda c, g=g: F_DV + (2 * g + dh_(c)) * 128)
            bpr = NDB // d
            qR = qT[:, 1, 0:HS]
            kR = nkT[:, 0:3 * RK]
            vR = kT[:, 1, 0:3 * RK]
            P.op(V_, lambda e, d=d: e.tensor_copy(out=qR.rearrange("p (r n) -> p r n", r=d), in_=qT[:, 0, 0:HS].rearrange("p (n r) -> p r n", r=d)), R=[qT], W=[qT])
            P.op(G_, lambda e, d=d: e.tensor_copy(out=kR.rearrange("p (r n) -> p r n", r=d), in_=kT[:, 0, 0:3 * RK].rearrange("p (n r) -> p r n", r=d)), R=[kT], W=[nkT])
            P.op(S_, lambda e, d=d: e.activation(out=vR.rearrange("p (r n) -> p r n", r=d), in_=vT[:, 0, 0:3 * RK].rearrange("p (n r) -> p r n", r=d), func=AF.Copy), R=[vT], W=[kT])

            def kslice(blk, half, d=d, bpr=bpr):
                r, nb = blk // bpr, blk % bpr
                st_ = r * (3 * RK // d) + RK // d + nb * 128 - 128 + half * 128
                return slice(st_, st_ + 128)

            def qslice(blk, d=d, bpr=bpr):
                r, nb = blk // bpr, blk % bpr
                st_ = r * (HS // d) + nb * 128
                return slice(st_, st_ + 128)

            def qpos(blk, d=d, bpr=bpr):
                r, nb = blk // bpr, blk % bpr
                st_ = nb * 128 * d + r
                return slice(st_, st_ + 127 * d + 1, d)

            v_transposes(NDB * 2, lambda i: kslice(i // 2, i % 2), src=vR, src_trk=kT)
            if dbg is not None and g == 1:
                P.dma("sync", dbg["q"][:], qT[:, :, 0:HS], R=[qT], W=[dbg["q"]], sb=qT)
                P.dma("sync", dbg["k"][:], kT[:, :, 0:3 * RK], R=[kT], W=[dbg["k"]], sb=kT)
                P.dma("sync", dbg["kr"][:], nkT[:, 0:3 * RK], R=[nkT], W=[dbg["kr"]], sb=nkT)
                P.dma("sync", dbg["vv"][:], vv[:], R=[vv], W=[dbg["vv"]], sb=vv)
                for kk_ in ("q", "k", "kr", "vv"):
                    P.mark_output(dbg[kk_])
            for b4 in range(NDB // 4):
                i = cnt["t"]
                cnt["t"] += 1
                psp, psc = psS[0], psS[1]
                ptp, ptc = pT[0], pT[1]
                pn, pd = psN[i % 2], psD[i % 2]
                for u in range(4):
                    blk = b4 * 4 + u
                    us = slice(u * 128, (u + 1) * 128)
                    ks0, ks1, qsl = kslice(blk, 0), kslice(blk, 1), qslice(blk)
                    P.op(T_, lambda e, us=us, ks0=ks0, qsl=qsl: e.matmul(psp[:, us], lhsT=kR[:, ks0], rhs=qR[:, qsl], start=True, stop=True), R=[nkT, qT], W=[psp])
                    P.op(T_, lambda e, us=us, ks1=ks1, qsl=qsl: e.matmul(psc[:, us], lhsT=kR[:, ks1], rhs=qR[:, qsl], start=True, stop=True), R=[nkT, qT], W=[psc])
                P.op(S_, lambda e: e.activation(out=ptp[:], in_=psp[:], func=AF.Exp, scale=dscale), R=[psp], W=[ptp])
                P.op(S_, lambda e: e.activation(out=ptc[:], in_=psc[:], func=AF.Exp, scale=dscale), R=[psc], W=[ptc])
                for u in range(4):
                    blk = b4 * 4 + u
                    us = slice(u * 128, (u + 1) * 128)
                    mi = 2 if (blk % bpr == 0) else 0
                    P.op(V_, lambda e, us=us, mi=mi: e.tensor_tensor(out=ptp[:, us], in0=ptp[:, us], in1=dm[:, mi, :], op=ALU.mult), R=[ptp, dm], W=[ptp])
                P.op(G_, lambda e: e.tensor_tensor(out=ptc[:].rearrange("p (u q) -> p u q", u=4), in0=ptc[:].rearrange("p (u q) -> p u q", u=4),
                                                   in1=dm[:, 1, :].unsqueeze(1).to_broadcast([128, 4, 128]), op=ALU.mult), R=[ptc, dm], W=[ptc])
                for u in range(4):
                    blk = b4 * 4 + u
                    us = slice(u * 128, (u + 1) * 128)
                    P.op(T_, lambda e, blk=blk, us=us, pn=pn: e.matmul(pn[:, us], lhsT=vv[:, blk * 2, :], rhs=ptp[:, us], start=True, stop=False), R=[vv, ptp], W=[pn])
                    P.op(T_, lambda e, blk=blk, us=us, pn=pn: e.matmul(pn[:, us], lhsT=vv[:, blk * 2 + 1, :], rhs=ptc[:, us], start=False, stop=True), R=[vv, ptc], W=[pn], pe_acc=True)
                    P.op(T_, lambda e, us=us, pd=pd: e.matmul(pd[:, us], lhsT=ones[:], rhs=ptp[:, us], start=True, stop=False), R=[ones, ptp], W=[pd])
                    P.op(T_, lambda e, us=us, pd=pd: e.matmul(pd[:, us], lhsT=ones[:], rhs=ptc[:, us], start=False, stop=True), R=[ones, ptc], W=[pd], pe_acc=True)
                for u in range(4):
                    blk = b4 * 4 + u
                    us = slice(u * 128, (u + 1) * 128)
                    qs_ = qpos(blk)
                    dst_n = numS[:, qs_]
                    dst_d = denS[:, qs_]
                    if g == 0:
                        P.op(V_, lambda e, dst_n=dst_n, pn=pn, us=us: e.tensor_copy(out=dst_n, in_=pn[:, us]), R=[pn], W=[numS])
                        P.op(V_, lambda e, dst_d=dst_d, pd=pd, us=us: e.tensor_copy(out=dst_d, in_=pd[:, us]), R=[pd], W=[denS])
                    else:
                        P.op(V_, lambda e, dst_n=dst_n, pn=pn, us=us: e.tensor_tensor(out=dst_n, in0=dst_n, in1=pn[:, us], op=ALU.add), R=[pn, numS], W=[numS])
                        P.op(V_, lambda e, dst_d=dst_d, pd=pd, us=us: e.tensor_tensor(out=dst_d, in0=dst_d, in1=pd[:, us], op=ALU.add), R=[pd, denS], W=[denS])
        if dbg is not None:
            P.dma("sync", dbg["num"][:], numS[:], R=[numS], W=[dbg["num"]], sb=numS)
            P.dma("sync", dbg["den"][:], denS[:], R=[denS], W=[dbg["den"]], sb=denS)
            P.mark_output(dbg["num"])
            P.mark_output(dbg["den"])
        for c in range(HS // 512):
            qs = slice(c * 512, (c + 1) * 512)
            o = ob[c % 2]
            P.op(V_, lambda e, qs=qs: e.reciprocal(out=rden[:], in_=denS[:, qs]), R=[denS], W=[rden])
            P.op(V_, lambda e, o=o, qs=qs: e.tensor_tensor(out=o[:], in0=numS[:, qs], in1=rden[:], op=ALU.mult), R=[numS, rden], W=[o])
            P.dma("sync", own2[(16 + c // 4) * 128:(17 + c // 4) * 128, (c % 4) * 512:(c % 4 + 1) * 512], o[:], R=[o], W=[own2.sub(("dl", c))], sb=o)


import math

V_, G_, S_, T_ = "vector", "gpsimd", "scalar", "tensor"
D = 2048
EPS = 1e-6
NE = 16384


def stage_c(P, NT, dr, G2v, xin, xout, NEX=NE):
    GT = 2
    GW = GT * 128
    NG = NT // GW
    NCH = NEX // 128
    EG = 4
    x, out = xin, xout
    w_in, g_mix, g_ffn = dr["w_in"], dr["g_mix"], dr["g_ffn"]
    w_bsb, w_bdl, w_bml, w_out, w_q = dr["w_bsb"], dr["w_bdl"], dr["w_bml"], dr["w_out"], dr["w_q"]
    subkT, puT, pv = dr["subkT"], dr["puT"], dr["pv"]
    pr = np.arange(128)
    OOB = 1 << 30

    def g2row(core, grp, tok0):
        return (core * ROWS2 + grp * 128 + pr) * 8 + tok0 // GW

    ident = P.sbuf("ident", [128, 128], BF16)
    P.op(G_, lambda e: e.memset(ident[:], 1.0), W=[ident])
    P.op(G_, lambda e: e.affine_select(out=ident[:], in_=ident[:], pattern=[[-1, 128]], compare_op=ALU.is_equal,
                                       fill=0.0, base=0, channel_multiplier=1), R=[ident], W=[ident])
    skT = P.sbuf("skT", [128, 2, 128], BF16)
    P.dma("gpsimd", skT[:], subkT[:], R=[], W=[skT], sb=skT)
    grep = P.sbuf("grep", [128, D], F32)
    xh = [P.sbuf("xh%d" % i, [128, D], F32) for i in range(GT)]
    mg = [P.sbuf("mg%d" % i, [128, D], F32) for i in range(GT)]
    ub = P.sbuf("ub", [128, D], BF16)
    st = P.sbuf("st", [128, 16], F32)
    aT = P.sbuf("aT", [128, 16, GW], BF16)
    obT = P.sbuf("obT", [128, 12, GW], BF16)
    wsl = [P.sbuf("wsl%d" % i, [128, 16, 512], BF16) for i in range(2)]
    vsl = [P.sbuf("vsl%d" % i, [128, EG, D], BF16) for i in range(1)]
    pbs = P.sbuf("pbs", [128, 12, 512], BF16)
    gsig = P.sbuf("gsig", [128, 512], F32)
    qT = P.sbuf("qT", [128, 16, GW], BF16)
    s_sb = P.sbuf("s_sb", [128, 16, 128], F32)
    sv = P.sbuf("sv", [128, 16, 16], F32)
    tmp128 = P.sbuf("tmp128", [128, 128], F32)
    cand = P.sbuf("cand", [128, 8, 256], F32)
    cand2 = P.sbuf("cand2", [128, 256], F32)
    best = P.sbuf("best", [128, 8, 16], F32)
    bex = P.sbuf("bex", [128, 8, 16], F32)
    zz = P.sbuf("zz", [128, 8], F32)
    e0f = P.sbuf("e0f", [128, 8, 128], F32)
    thr = [P.sbuf("thr%d" % i, [128, 8, 128], F32) for i in range(GT)]
    s1 = [P.sbuf("s1_%d" % i, [128, 8, 128], F32) for i in range(GT)]
    e0 = [P.sbuf("e0_%d" % i, [128, 8, 128], BF16) for i in range(GT)]
    e1 = [P.sbuf("e1_%d" % i, [128, 8, 128], BF16) for i in range(GT)]
    Mb = P.sbuf("Mb", [128, 8, EG, 128], BF16)
    Gb = P.sbuf("Gb", [128, GT, EG, 128], BF16)
    gA = [P.sbuf("gA%d" % i, [128, GW], F32) for i in range(2)]
    HT = [P.sbuf("HT%d" % i, [128, EG, GW], BF16) for i in range(2)]
    psT = P.psum("psT", [128, 1024], BF16)
    psA = [P.psum("psA%d" % i, [128, 512], F32) for i in range(2)]
    psB = P.psum("psB", [128, 512], F32)
    psY = [P.psum("psY%d" % i, [128, 512], F32) for i in range(GT)]
    psG = P.psum("psG", [128, 512], BF16)

    def rstd_from_ss(ss_ap, n, trk):
        P.op(V_, lambda e: e.tensor_scalar(out=ss_ap, in0=ss_ap, scalar1=1.0 / n, scalar2=EPS, op0=ALU.mult, op1=ALU.add), R=[trk], W=[trk])
        P.op(S_, lambda e: e.activation(out=ss_ap, in_=ss_ap, func=AF.Sqrt), R=[trk], W=[trk])
        P.op(V_, lambda e: e.reciprocal(out=ss_ap, in_=ss_ap), R=[trk], W=[trk])

    def norm_transpose(src, tt):
        P.op(S_, lambda e: e.activation(out=ub[:], in_=src[:], func=AF.Square, accum_out=st[:, 0:1]), R=[src], W=[ub, st])
        rstd_from_ss(st[:, 0:1], D, st)
        P.op(V_, lambda e: e.scalar_tensor_tensor(out=ub[:], in0=src[:], scalar=st[:, 0:1], in1=grep[:], op0=ALU.mult, op1=ALU.mult), R=[src, st, grep], W=[ub])
        transpose_ub(tt)

    def transpose_ub(tt):
        for half in range(2):
            for j in range(8):
                c = half * 8 + j
                P.op(T_, lambda e, c=c, j=j: e.transpose(psT[:, j * 128:(j + 1) * 128], ub[:, c * 128:(c + 1) * 128], ident[:]), R=[ub, ident], W=[psT])
            P.op(V_, lambda e, half=half, tt=tt: e.tensor_copy(out=aT[:, half * 8:(half + 1) * 8, tt * 128:(tt + 1) * 128],
                                                            in_=psT[:].rearrange("p (j t) -> p j t", j=8)), R=[psT], W=[aT.sub(tt)])

    ws = {"n": 0}

    def load_slab(src_ap, kc=16):
        wb = wsl[ws["n"] % 2]
        ws["n"] += 1
        P.dma("gpsimd", wb[:, 0:kc, :], src_ap, R=[], W=[wb], sb=wb)
        return wb

    AT_ALL = [aT.sub(t) for t in range(GT)]

    class _Slices(dict):
        def __missing__(self, key):
            b_ = Buf("G2s", G2v.t[:, key[0]:key[0] + GW])
            b_.trk = G2v.trk
            self[key] = b_
            return b_
    G2v_slices = _Slices()
    for g in range(NG):
        c0 = g * GW
        P.dma("sync", grep[:], g_mix[0:1, :].partition_broadcast(128), R=[], W=[grep], sb=grep)
        for tt in range(GT):
            r0 = c0 + tt * 128
            P.dma("sync", xh[tt][:], x[r0:r0 + 128, :], R=[], W=[xh[tt]], sb=xh[tt])
            norm_transpose(xh[tt], tt)
        hh0, col0 = c0 // 1024, c0 % 1024
        qch = c0 // 512
        colp = (qch // 2) * 512 + (c0 % 512)
        nrows2 = 8 * ROWS2 * 8
        for k in range(4):
            P.gather(obT[:, k, :], obT.sub(0), G2v,
                     (lambda c, k=k, c0=c0: g2row(4 * (c // 4) + k, 0 + c % 4, c0)))
        for k in range(2):
            P.gather(obT[:, 4 + k, :], obT.sub(1), G2v,
                     (lambda c, k=k, c0=c0: g2row(4 * (c // 4) + 2 * k + (c % 4) // 2, 16 + (c % 4) % 2, c0)))
        for k in range(6):
            def w_of(c, k=k, qch=qch):
                return 2 * ((c // 4) * 6 + k) + (qch % 2)
            P.gather(obT[:, 6 + k, :], obT.sub(2), G2v,
                     (lambda c, w_of=w_of, colp=colp: g2row(w_of(c) % 8, 4 + 4 * (w_of(c) // 8) + c % 4, colp)))
        OB = [obT.sub(0), obT.sub(1), obT.sub(2)]
        for sl in range(4):
            cs = slice(sl * 512, (sl + 1) * 512)
            P.dma("gpsimd", pbs[:, 0:4, :], w_bsb[:, cs].rearrange("(k p) c -> p k c", p=128), R=[], W=[pbs.sub(0)], sb=pbs.sub(0))
            P.dma("gpsimd", pbs[:, 4:6, :], w_bdl[:, cs].rearrange("(k p) c -> p k c", p=128), R=[], W=[pbs.sub(1)], sb=pbs.sub(1))
            P.dma("gpsimd", pbs[:, 6:12, :], w_bml[:, cs].rearrange("(k p) c -> p k c", p=128), R=[], W=[pbs.sub(2)], sb=pbs.sub(2))
            for b, (k0, k1) in enumerate(((0, 4), (4, 6), (6, 12))):
                wb = load_slab(w_in[:, 4672 + b * D + sl * 512:4672 + b * D + (sl + 1) * 512].rearrange("(k p) c -> p k c", p=128))
                for tt in range(GT):
                    ts = slice(tt * 128, (tt + 1) * 128)
                    pa = psA[tt % 2]
                    for k in range(16):
                        P.op(T_, lambda e, pa=pa, k=k, ts=ts, wb=wb: e.matmul(pa[:], lhsT=aT[:, k, ts], rhs=wb[:, k, :], start=(k == 0), stop=(k == 15)),
                             R=[aT.sub(tt), wb], W=[pa], pe_acc=(k > 0))
                    for k in range(k0, k1):
                        P.op(T_, lambda e, k=k, ts=ts, k0=k0, k1=k1: e.matmul(psB[:], lhsT=obT[:, k, ts], rhs=pbs[:, k, :], start=(k == k0), stop=(k == k1 - 1)),
                             R=[OB[b], pbs.sub(b)], W=[psB], pe_acc=(k > k0))
                    P.op(S_, lambda e, pa=pa: e.activation(out=gsig[:], in_=pa[:], func=AF.Sigmoid), R=[pa], W=[gsig])
                    if b == 0:
                        P.op(V_, lambda e, tt=tt, cs=cs: e.tensor_tensor(out=mg[tt][:, cs], in0=gsig[:], in1=psB[:], op=ALU.mult), R=[gsig, psB], W=[mg[tt].sub(sl)])
                    else:
                        P.op(V_, lambda e: e.tensor_tensor(out=gsig[:], in0=gsig[:], in1=psB[:], op=ALU.mult), R=[gsig, psB], W=[gsig])
                        P.op(V_, lambda e, tt=tt, cs=cs: e.tensor_tensor(out=mg[tt][:, cs], in0=mg[tt][:, cs], in1=gsig[:], op=ALU.add), R=[gsig, mg[tt].sub(sl)], W=[mg[tt].sub(sl)])
        for tt in range(GT):
            P.op(S_, lambda e, tt=tt: e.activation(out=ub[:], in_=mg[tt][:], func=AF.Copy), R=[mg[tt].sub(s_) for s_ in range(4)], W=[ub])
            transpose_ub(tt)
        for sl in range(4):
            cs = slice(sl * 512, (sl + 1) * 512)
            wb = load_slab(w_out[:, cs].rearrange("(k p) c -> p k c", p=128))
            for tt in range(GT):
                ts = slice(tt * 128, (tt + 1) * 128)
                pa = psA[tt % 2]
                for k in range(16):
                    P.op(T_, lambda e, pa=pa, k=k, ts=ts, wb=wb: e.matmul(pa[:], lhsT=aT[:, k, ts], rhs=wb[:, k, :], start=(k == 0), stop=(k == 15)),
                         R=[aT.sub(tt), wb], W=[pa], pe_acc=(k > 0))
                P.op(V_, lambda e, tt=tt, cs=cs, pa=pa: e.tensor_tensor(out=xh[tt][:, cs], in0=xh[tt][:, cs], in1=pa[:], op=ALU.add), R=[pa, xh[tt]], W=[xh[tt]])
        P.dma("sync", grep[:], g_ffn[0:1, :].partition_broadcast(128), R=[], W=[grep], sb=grep)
        for tt in range(GT):
            norm_transpose(xh[tt], tt)
        for sl in range(4):
            wb = load_slab(w_q[:, sl * 512:(sl + 1) * 512].rearrange("(k p) c -> p k c", p=128))
            for cc in range(4):
                hp = sl * 4 + cc
                pa = psA[hp % 2]
                for k in range(16):
                    P.op(T_, lambda e, pa=pa, k=k, wb=wb, cc=cc: e.matmul(pa[:, 0:GW], lhsT=wb[:, k, cc * 128:(cc + 1) * 128], rhs=aT[:, k, :], start=(k == 0), stop=(k == 15)),
                         R=AT_ALL + [wb], W=[pa], pe_acc=(k > 0))
                P.op(S_, lambda e, pa=pa, hp=hp: e.activation(out=qT[:, hp, :], in_=pa[:, 0:GW], func=AF.Copy), R=[pa], W=[qT.sub(hp)])
        for tt in range(GT):
            ts = slice(tt * 128, (tt + 1) * 128)
            for q4 in range(4):
                pa = psA[q4 % 2]
                for u in range(4):
                    hp = q4 * 4 + u
                    P.op(T_, lambda e, pa=pa, hp=hp, u=u, ts=ts: e.matmul(pa[:, u * 128:(u + 1) * 128], lhsT=qT[:, hp, ts], rhs=skT[:, hp % 2, :], start=True, stop=True),
                         R=[qT.sub(hp), skT], W=[pa])
                P.op(S_, lambda e, pa=pa, q4=q4: e.activation(out=s_sb[:, q4 * 4:(q4 + 1) * 4, :], in_=pa[:].rearrange("p (u n) -> p u n", u=4), func=AF.Copy), R=[pa], W=[s_sb])
            for hp in range(16):
                P.op(V_, lambda e, hp=hp: e.max(out=sv[:, hp, 0:8], in_=s_sb[:, hp, :]), R=[s_sb], W=[sv])
                P.op(V_, lambda e, hp=hp: e.match_replace(out=tmp128[:], in_to_replace=sv[:, hp, 0:8], in_values=s_sb[:, hp, :], imm_value=-1e30), R=[s_sb, sv], W=[tmp128])
                P.op(V_, lambda e, hp=hp: e.max(out=sv[:, hp, 8:16], in_=tmp128[:]), R=[tmp128], W=[sv])
            svv = sv[:].rearrange("p (h two) k -> p h two k", two=2)
            P.op(V_, lambda e: e.tensor_tensor(out=cand[:].rearrange("p h (a b) -> p h a b", a=16),
                                               in0=svv[:, :, 0, :].unsqueeze(3).to_broadcast([128, 8, 16, 16]),
                                               in1=svv[:, :, 1, :].unsqueeze(2).to_broadcast([128, 8, 16, 16]), op=ALU.add), R=[sv], W=[cand])
            for h in range(8):
                P.op(V_, lambda e, h=h: e.max(out=best[:, h, 0:8], in_=cand[:, h, :]), R=[cand], W=[best])
                P.op(V_, lambda e, h=h: e.match_replace(out=cand2[:], in_to_replace=best[:, h, 0:8], in_values=cand[:, h, :], imm_value=-1e30), R=[cand, best], W=[cand2])
                P.op(V_, lambda e, h=h: e.max(out=best[:, h, 8:16], in_=cand2[:]), R=[cand2], W=[best])
            P.op(V_, lambda e: e.tensor_tensor(out=bex[:], in0=best[:], in1=best[:, :, 0:1].to_broadcast([128, 8, 16]), op=ALU.subtract), R=[best], W=[bex])
            P.op(S_, lambda e: e.activation(out=bex[:], in_=bex[:], func=AF.Exp), R=[bex], W=[bex])
            P.op(V_, lambda e: e.reduce_sum(out=zz[:], in_=bex[:], axis=AX.X), R=[bex], W=[zz])
            P.op(V_, lambda e: e.reciprocal(out=zz[:], in_=zz[:]), R=[zz], W=[zz])
            sall = s_sb[:].rearrange("p (h two) n -> p h two n", two=2)
            s0v, s1v = sall[:, :, 0, :], sall[:, :, 1, :]
            P.op(V_, lambda e: e.tensor_tensor(out=e0f[:], in0=s0v, in1=svv[:, :, 0, 0:1].to_broadcast([128, 8, 128]), op=ALU.subtract), R=[s_sb, sv], W=[e0f])
            P.op(S_, lambda e: e.activation(out=e0f[:], in_=e0f[:], func=AF.Exp), R=[e0f], W=[e0f])
            P.op(V_, lambda e, tt=tt: e.tensor_tensor(out=e0[tt][:], in0=e0f[:], in1=zz[:].unsqueeze(2).to_broadcast([128, 8, 128]), op=ALU.mult), R=[e0f, zz], W=[e0[tt]])
            P.op(V_, lambda e: e.tensor_tensor(out=e0f[:], in0=s1v, in1=svv[:, :, 1, 0:1].to_broadcast([128, 8, 128]), op=ALU.subtract), R=[s_sb, sv], W=[e0f])
            P.op(S_, lambda e, tt=tt: e.activation(out=e1[tt][:], in_=e0f[:], func=AF.Exp), R=[e0f], W=[e1[tt]])
            P.op(V_, lambda e, tt=tt: e.tensor_tensor(out=thr[tt][:], in0=best[:, :, 15:16].to_broadcast([128, 8, 128]), in1=s0v, op=ALU.subtract), R=[s_sb, best], W=[thr[tt]])
            P.op(V_, lambda e, tt=tt: e.tensor_scalar(out=thr[tt][:], in0=thr[tt][:], scalar1=-1e-6, scalar2=None, op0=ALU.add), R=[thr[tt]], W=[thr[tt]])
            P.op(V_, lambda e, tt=tt: e.tensor_copy(out=s1[tt][:], in_=s1v), R=[s_sb], W=[s1[tt]])
        for eg in range(NCH // EG):
            e_lo = eg * EG * 128
            ut = load_slab(puT[:, e_lo:e_lo + EG * 128].rearrange("(k p) c -> p k c", p=128))
            vb = vsl[0]
            P.dma("gpsimd", vb[:], pv[e_lo:e_lo + EG * 128, :].rearrange("(i p) d -> p i d", p=128), R=[], W=[vb], sb=vb)
            ht = HT[eg % 2]
            for tt in range(GT):
                isl = slice(eg * EG, (eg + 1) * EG)
                P.op(V_, lambda e, tt=tt, isl=isl: e.tensor_tensor(out=Mb[:], in0=s1[tt][:].unsqueeze(2).to_broadcast([128, 8, EG, 128]),
                                                                   in1=thr[tt][:, :, isl].unsqueeze(3).to_broadcast([128, 8, EG, 128]), op=ALU.is_ge), R=[s1[tt], thr[tt]], W=[Mb])
                P.op(V_, lambda e, tt=tt: e.tensor_tensor(out=Mb[:], in0=Mb[:], in1=e1[tt][:].unsqueeze(2).to_broadcast([128, 8, EG, 128]), op=ALU.mult), R=[Mb, e1[tt]], W=[Mb])
                P.op(V_, lambda e, tt=tt, isl=isl: e.tensor_tensor(out=Mb[:], in0=Mb[:], in1=e0[tt][:, :, isl].unsqueeze(3).to_broadcast([128, 8, EG, 128]), op=ALU.mult), R=[Mb, e0[tt]], W=[Mb])
                def _red(e, tt=tt):
                    with P.nc.allow_low_precision("sum of <=8 sparse bf16 gate terms"):
                        return e.tensor_reduce(out=Gb[:, tt], in_=Mb[:].rearrange("p h i j -> p i j h"), axis=AX.X, op=ALU.add)
                P.op(V_, _red, R=[Mb], W=[Gb.sub(tt)])
            for il in range(EG):
                pa = psA[il % 2]
                ga = gA[il % 2]
                for k in range(16):
                    P.op(T_, lambda e, pa=pa, k=k, il=il, ut=ut: e.matmul(pa[:, 0:GW], lhsT=ut[:, k, il * 128:(il + 1) * 128], rhs=aT[:, k, :], start=(k == 0), stop=(k == 15)),
                         R=AT_ALL + [ut], W=[pa], pe_acc=(k > 0))
                P.op(S_, lambda e, pa=pa, ga=ga: e.activation(out=ga[:], in_=pa[:, 0:GW], func=AF.Gelu), R=[pa], W=[ga])
                for tt in range(GT):
                    P.op(T_, lambda e, tt=tt, il=il: e.transpose(psG[:, tt * 128:(tt + 1) * 128], Gb[:, tt, il, :], ident[:]), R=[Gb.sub(tt), ident], W=[psG])
                P.op(V_, lambda e, il=il, ga=ga, ht=ht: e.tensor_tensor(out=ht[:, il, :], in0=ga[:], in1=psG[:, 0:GW], op=ALU.mult), R=[ga, psG], W=[ht.sub(il)])
            for q4 in range(4):
                for tt in range(GT):
                    py = psY[tt]
                    for il in range(EG):
                        P.op(T_, lambda e, py=py, il=il, tt=tt, q4=q4, ht=ht, vb=vb: e.matmul(py[:], lhsT=ht[:, il, tt * 128:(tt + 1) * 128], rhs=vb[:, il, q4 * 512:(q4 + 1) * 512],
                                                                                           start=(il == 0), stop=(il == EG - 1)),
                             R=[ht.sub(il), vb], W=[py], pe_acc=(il > 0))
                    P.op(V_, lambda e, py=py, tt=tt, q4=q4: e.tensor_tensor(out=xh[tt][:, q4 * 512:(q4 + 1) * 512], in0=xh[tt][:, q4 * 512:(q4 + 1) * 512], in1=py[:], op=ALU.add),
                         R=[py, xh[tt]], W=[xh[tt]])
        for tt in range(GT):
            r0 = c0 + tt * 128
            P.dma("sync", out[r0:r0 + 128, :], xh[tt][:], R=[xh[tt]], W=[out.sub(r0)], sb=xh[tt])


import numpy as np
import ml_dtypes
BF = ml_dtypes.bfloat16


def diag_masks(strict):
    k = np.arange(128)[:, None, None] + 128 * np.arange(4)[None, :, None]
    q = np.arange(512)[None, None, :]
    return ((k < q) if strict else (k <= q)).astype(BF)


import ml_dtypes

LAYER_W = (("w_in", [2048, 10816]), ("g_mix", [1, 2048]), ("g_dq", [1, 128]), ("g_dk", [1, 128]), ("g_cq", [1, 512]), ("g_ckv", [1, 256]),
           ("g_mq", [1, 192]), ("g_mk", [1, 192]), ("w_uq", [512, 1152]), ("w_ukv", [256, 1536]), ("w_bsb", [512, 2048]), ("w_bdl", [256, 2048]),
           ("w_bml", [768, 2048]), ("w_out", [2048, 2048]), ("g_ffn", [1, 2048]), ("w_q", [2048, 2048]), ("subkT", [128, 2, 128]),
           ("puT", [2048, 16384]), ("pv", [16384, 2048]))


def build_fused(S=8192, depth=2, upto="c"):
    P = Prog2()
    nc = P.nc
    NT = S // 4
    dr = {}
    dr["x"] = P.dram("x", [NT, 2048], F32, "ExternalInput")
    dr["pos"] = P.dram("pos", [NT, 1], I32, "ExternalInput")
    idx_d = P.dram("idx", [128, NIDX], I32, "ExternalInput")
    for nm, shp in (("msd", [128, 4, 512]), ("md", [128, 4, 512]), ("mka", [128, 4, 512]), ("mkb", [128, 4, 512]), ("dmask", [128, 3, 128])):
        dr[nm] = P.dram(nm, shp, BF16, "ExternalInput")
    dr["inv64"] = P.dram("inv64", [1, 64], F32, "ExternalInput")
    dr["inv32"] = P.dram("inv32", [1, 32], F32, "ExternalInput")
    lw = []
    for l in range(depth):
        lw.append({nm: P.dram("%s_%d" % (nm, l), ([shp[0] // 128, shp[1]] if (upto != "c" and nm == "pv") else ([shp[0], shp[1] // 128] if (upto != "c" and nm == "puT") else shp)), F32, "ExternalInput") for nm, shp in LAYER_W})
    out = P.dram("out", [NT, 2048], F32, "ExternalOutput")
    xmid = [Buf("xmid%d" % l, nc.dram_tensor("xmid%d" % l, [NT, 2048], F32, kind="Internal").ap()) for l in range(depth - 1)]
    P.idx_sb = P.sbuf("idx_sb", [128, NIDX], I32)
    P.dma("sync", P.idx_sb[:], idx_d[:], R=[], W=[P.idx_sb], sb=P.idx_sb)
    mark = P.mark()
    for l in range(depth):
        own1 = Buf("own1_%d" % l, nc.dram_tensor("own1_%d" % l, [ROWS1, NT], BF16, kind="Internal").ap())
        G1 = Buf("G1_%d" % l, nc.dram_tensor("G1_%d" % l, [8 * ROWS1, NT], BF16, kind="Internal", addr_space="Shared").ap())
        own2 = Buf("own2_%d" % l, nc.dram_tensor("own2_%d" % l, [ROWS2, NT], BF16, kind="Internal").ap())
        G2 = Buf("G2_%d" % l, nc.dram_tensor("G2_%d" % l, [8 * ROWS2, NT], BF16, kind="Internal", addr_space="Shared").ap())
        pay1 = Buf("pay1_%d" % l, own1.t[0:6912, :])
        payP = Buf("payP_%d" % l, own1.t[6912:ROWS1, :])
        d = dict(dr)
        d.update(lw[l])
        d["x"] = dr["x"] if l == 0 else xmid[l - 1]
        stage_a(P, NT, d, pay1, payP)
        if upto == "a":
            break
        P.all_gather(own1, G1, extra_R=list(pay1.subs.values()) + list(payP.subs.values()))
        P.reset(mark)
        if upto == "ag1":
            dbg_own = P.dram("dbg_own", [ROWS1, NT], BF16, "ExternalOutput")
            dbg_g = P.dram("dbg_g", [ROWS1, NT], BF16, "ExternalOutput")
            dbg_q = P.dram("dbg_q", [128, 2, 1024], BF16, "ExternalOutput")
            scr = P.sbuf("scr", [128, 16, NT], BF16)
            G1v = Buf("G1v_%d" % l, G1.t.rearrange("r (h c) -> (r h) c", h=2))
            G1v.trk = G1.trk
            for i0_ in range(0, ROWS1, 2048):
                P.dma("sync", scr[:], own1.t[i0_:i0_ + 2048, :].rearrange("(a p) t -> p a t", p=128), R=[own1.trk] + list(pay1.subs.values()) + list(payP.subs.values()), W=[scr], sb=scr) if i0_ + 2048 <= ROWS1 else None
                if i0_ + 2048 <= ROWS1:
                    P.dma("sync", dbg_own[i0_:i0_ + 2048, :].rearrange("(a p) t -> p a t", p=128), scr[:], R=[scr], W=[dbg_own.sub(i0_)], sb=scr)
                    P.mark_output(dbg_own.sub(i0_))
                    P.dma("gpsimd", scr[:], G1.t[3 * ROWS1 + i0_:3 * ROWS1 + i0_ + 2048, :].rearrange("(a p) t -> p a t", p=128), R=[G1.trk], W=[scr], sb=scr)
                    P.dma("sync", dbg_g[i0_:i0_ + 2048, :].rearrange("(a p) t -> p a t", p=128), scr[:], R=[scr], W=[dbg_g.sub(i0_)], sb=scr)
                    P.mark_output(dbg_g.sub(i0_))
            pr_ = np.arange(128)
            for hh in range(2):
                P.gather(scr[:, hh, 0:1024], scr, G1v, (lambda c, hh=hh: g1row(4 * (c // 4) + 1, F_SBQ + (c % 4) * 128 + pr_, hh)))
            P.dma("sync", dbg_q[:], scr[:, 0:2, 0:1024], R=[scr], W=[dbg_q], sb=scr)
            P.mark_output(dbg_q)
            break
        G1v = Buf("G1v_%d" % l, G1.t.rearrange("r (h c) -> (r h) c", h=2))
        G1v.trk = G1.trk
        dbg = None
        if upto == "ag2":
            dbg = {"q": P.dram("dbg_q", [128, 2, S // 2], BF16, "ExternalOutput"), "k": P.dram("dbg_k", [128, 2, 3 * NT], BF16, "ExternalOutput"),
                   "kr": P.dram("dbg_kr", [128, 3 * NT], BF16, "ExternalOutput"), "vv": P.dram("dbg_vv", [128, S // 128, 128], BF16, "ExternalOutput"),
                   "num": P.dram("dbg_num", [128, S // 2], F32, "ExternalOutput"), "den": P.dram("dbg_den", [128, S // 2], F32, "ExternalOutput")}
        stage_b(P, S, d, G1v, own2, dbg)
        if upto == "b":
            break
        P.all_gather(own2, G2, extra_R=list(own2.subs.values()))
        P.reset(mark)
        if upto == "ag2":
            dbg_own2 = P.dram("dbg_own2", [ROWS2, NT], BF16, "ExternalOutput")
            dbg_ob = P.dram("dbg_ob", [128, 2, 12, 256], BF16, "ExternalOutput")
            scr = P.sbuf("scr", [128, 18, NT], BF16)
            P.dma("sync", scr[:], own2.t[:, :].rearrange("(a p) t -> p a t", p=128), R=[own2.trk] + list(own2.subs.values()), W=[scr], sb=scr)
            P.dma("sync", dbg_own2[:, :].rearrange("(a p) t -> p a t", p=128), scr[:], R=[scr], W=[dbg_own2], sb=scr)
            P.mark_output(dbg_own2)
            G2v = Buf("G2v_%d" % l, G2.t.rearrange("r (h c) -> (r h) c", h=8))
            G2v.trk = G2.trk
            obd = P.sbuf("obd", [128, 2, 12, 256], BF16)
            pr = np.arange(128)
            GW = 256
            def g2row(core, grp, tok0):
                return (core * ROWS2 + grp * 128 + pr) * 8 + tok0 // GW
            for gi, c0 in enumerate((0, 1280)):
                qch = c0 // 512
                colp = (qch // 2) * 512 + (c0 % 512)
                for k in range(4):
                    P.gather(obd[:, gi, k, :], obd, G2v, (lambda c, k=k, c0=c0: g2row(4 * (c // 4) + k, 0 + c % 4, c0)))
                for k in range(2):
                    P.gather(obd[:, gi, 4 + k, :], obd, G2v, (lambda c, k=k, c0=c0: g2row(4 * (c // 4) + 2 * k + (c % 4) // 2, 16 + (c % 4) % 2, c0)))
                for k in range(6):
                    def w_of(c, k=k, qch=qch):
                        return 2 * ((c // 4) * 6 + k) + (qch % 2)
                    P.gather(obd[:, gi, 6 + k, :], obd, G2v, (lambda c, w_of=w_of, colp=colp: g2row(w_of(c) % 8, 4 + 4 * (w_of(c) // 8) + c % 4, colp)))
            P.dma("sync", dbg_ob[:], obd[:], R=[obd], W=[dbg_ob], sb=obd)
            P.mark_output(dbg_ob)
            break
        G2v = Buf("G2v_%d" % l, G2.t.rearrange("r (h c) -> (r h) c", h=8))
        G2v.trk = G2.trk
        xout = out if l == depth - 1 else xmid[l]
        stage_c(P, NT, d, G2v, d["x"], xout, NEX=(16384 if upto == "c" else 128))
        P.reset(mark)
    for t in out.subs.values():
        P.mark_output(t)
    P.barrier()
    return P, P.build()


def kernel(x, positions, norm_mix, w_in, dil_q_norm, dil_k_norm, mla_cq_norm, mla_ckv_norm,
           mla_w_uq, mla_w_ukv, mla_q_norm, mla_k_norm, w_branch_sb, w_branch_dil, w_branch_mla,
           w_out, norm_ffn, peer_w_query, peer_sub_keys, peer_u, peer_v):
    f32 = np.float32
    B, S, Dm = 2, 8192, 2048
    NC, NT = 8, 2048
    A = lambda a: np.ascontiguousarray(np.asarray(a, f32))
    P, nc = build_fused(S, 2)
    xf = A(x).reshape(B * S, Dm)
    posf = np.ascontiguousarray(np.asarray(positions).astype(np.int32).reshape(B * S, 1))
    common = {}
    common["inv64"] = (10000.0 ** (-np.arange(64, dtype=f32) / 64)).reshape(1, 64).astype(f32)
    common["inv32"] = (10000.0 ** (-np.arange(32, dtype=f32) / 32)).reshape(1, 32).astype(f32)
    common["msd"] = diag_masks(True)
    common["md"] = diag_masks(False)
    for l in range(2):
        src = {"w_in": w_in[l], "g_mix": norm_mix[l], "g_dq": dil_q_norm[l], "g_dk": dil_k_norm[l], "g_cq": mla_cq_norm[l], "g_ckv": mla_ckv_norm[l],
               "g_mq": mla_q_norm[l], "g_mk": mla_k_norm[l], "w_uq": mla_w_uq[l], "w_ukv": mla_w_ukv[l], "w_bsb": w_branch_sb[l], "w_bdl": w_branch_dil[l],
               "w_bml": w_branch_mla[l], "w_out": w_out[l], "g_ffn": norm_ffn[l], "w_q": peer_w_query[l]}
        for nm, shp in LAYER_W:
            if nm in src:
                common["%s_%d" % (nm, l)] = A(src[nm]).reshape(shp)
        common["subkT_%d" % l] = np.ascontiguousarray(np.asarray(peer_sub_keys[l], f32).transpose(2, 0, 1))
        common["puT_%d" % l] = np.ascontiguousarray(np.asarray(peer_u[l], f32).T)
        common["pv_%d" % l] = A(peer_v[l])
    k = np.arange(128)[:, None]
    q = np.arange(128)[None, :]
    maps = []
    for c in range(NC):
        m = dict(common)
        m["x"] = xf[c * NT:(c + 1) * NT]
        m["pos"] = posf[c * NT:(c + 1) * NT]
        m["idx"] = P.idx_table(c)
        typ, hf = c % 2, c % 2
        if typ == 0:
            m["mka"], m["mkb"] = diag_masks(False), np.zeros((128, 4, 512), BF)
        else:
            m["mka"], m["mkb"] = np.ones((128, 4, 512), BF), diag_masks(False)
        dm = np.zeros((128, 3, 128), BF)
        dm[:, 0, :] = (k >= q)
        dm[:, 1, :] = (k <= q)
        dm[:, 2, :] = (k >= q) if hf == 1 else 0
        m["dmask"] = dm
        maps.append(m)
    res = run_bass_kernel_spmd(nc, maps, core_ids=list(range(NC)))
    o = np.concatenate([res.results[c]["out"] for c in range(NC)], axis=0).astype(f32)
    return o.reshape(B, S, Dm)
```
